# Optimizing a Trainium2 kernel written in Bass

```python
import math
import jax, jax.numpy as jnp
from jax import lax
import numpy as np

D_MODEL = 1024
BATCH = 8
SEQ = 4096
DEPTH = 2

N_MIXERS = 2
N_A = (DEPTH + 1) // 2
N_B = DEPTH // 2
HG_HEAD_DIM = 128
HG_HEADS = D_MODEL // HG_HEAD_DIM
HG_DIM = HG_HEADS * HG_HEAD_DIM
HG_CHUNK = 64
D_INNER = 2 * D_MODEL
SSM_HEAD_DIM = 64
SSM_HEADS = D_INNER // SSM_HEAD_DIM
SSM_GROUPS = 8
SSM_HPG = SSM_HEADS // SSM_GROUPS
SSM_STATE = 128
SSM_CONV = 5
SSD_CHUNK = 128
GN = SSM_GROUPS * SSM_STATE
CONV_DIM = D_INNER + 2 * GN
B_PROJ = 2 * D_INNER + 2 * GN + 2 * SSM_HEADS
D_FF = ((8 * D_MODEL // 3 + 255) // 256) * 256
FFN_CONV = 3
EPS = 1e-6

kernel_name = "hgrn2_mamba2_convglu_bidir_hybrid"


def rmsnorm(x, w):
    xf = x.astype(jnp.float32)
    y = xf * lax.rsqrt(jnp.mean(xf * xf, axis=-1, keepdims=True) + EPS)
    return (y * w.astype(jnp.float32)).astype(x.dtype)


def group_rmsnorm(x, w, groups):
    shp = x.shape
    xf = x.astype(jnp.float32).reshape(*shp[:-1], groups, shp[-1] // groups)
    y = xf * lax.rsqrt(jnp.mean(xf * xf, axis=-1, keepdims=True) + EPS)
    y = y.reshape(shp) * w.astype(jnp.float32)
    return y.astype(x.dtype)


def dwconv_centred(x, w, b):
    ch = x.shape[-1]
    y = lax.conv_general_dilated(x, w[:, None, :].astype(x.dtype), window_strides=(1,), padding='SAME',
                                 dimension_numbers=('NWC', 'WIO', 'NWC'), feature_group_count=ch)
    return y + b.astype(x.dtype)


def flip(t):
    return jnp.flip(t, axis=1)


def gla_chunked(q, k, v, logf):
    bsz, seq, h, dk = q.shape
    dv = v.shape[-1]
    nc = seq // HG_CHUNK

    def chunks(t):
        return t.reshape(bsz, nc, HG_CHUNK, h, t.shape[-1]).transpose(1, 0, 3, 2, 4)

    lower = jnp.tril(jnp.ones((HG_CHUNK, HG_CHUNK), dtype=bool))

    def step(state, inp):
        qc, kc, vc, gc = inp
        b = jnp.cumsum(gc, axis=2)
        o_inter = jnp.einsum('bhtk,bhkv->bhtv', qc * jnp.exp(b), state)
        rel = jnp.where(lower[:, :, None], b[:, :, :, None, :] - b[:, :, None, :, :], -jnp.inf)
        att = jnp.einsum('bhtsk,bhsk->bhts', qc[:, :, :, None, :] * jnp.exp(rel), kc)
        o = o_inter + jnp.einsum('bhts,bhsv->bhtv', att, vc)
        b_last = b[:, :, -1:, :]
        state = (jnp.exp(b_last)[:, :, 0, :, None] * state
                 + jnp.einsum('bhsk,bhsv->bhkv', kc * jnp.exp(b_last - b), vc))
        return state, o

    state0 = jnp.zeros((bsz, h, dk, dv), q.dtype)
    _, o = lax.scan(step, state0, (chunks(q), chunks(k), chunks(v), chunks(logf)))
    return o.transpose(1, 0, 3, 2, 4).reshape(bsz, seq, h, dv)


def hgrn2_mixer(u, w_in, lb, norm_w, w_out):
    bsz, seq, _ = u.shape
    q, f_fw, f_bw, iv, g = jnp.split(u @ w_in, 5, axis=-1)

    def heads(t):
        return t.reshape(bsz, seq, HG_HEADS, HG_HEAD_DIM)

    def gate(fr):
        f = lb + (1.0 - lb) * jax.nn.sigmoid(fr.astype(jnp.float32))
        return heads(jnp.log(f)).astype(u.dtype), heads(1.0 - f).astype(u.dtype)

    q = heads(jax.nn.silu(q))
    iv = heads(iv)
    logf_fw, k_fw = gate(f_fw)
    logf_bw, k_bw = gate(f_bw)
    o = gla_chunked(q, k_fw, iv, logf_fw) + flip(gla_chunked(flip(q), flip(k_bw), flip(iv), flip(logf_bw)))
    o = rmsnorm(o, norm_w) * jax.nn.silu(heads(g))
    return o.reshape(bsz, seq, HG_DIM) @ w_out


def ssd_chunked(x, dt, a, bm, cm):
    bsz, seq = x.shape[:2]
    nc = seq // SSD_CHUNK
    c = SSD_CHUNK
    xd = (x * dt[..., None]).reshape(bsz, nc, c, SSM_GROUPS, SSM_HPG, SSM_HEAD_DIM)
    la = (dt * a).reshape(bsz, nc, c, SSM_GROUPS, SSM_HPG).transpose(0, 3, 4, 1, 2)
    a_cum = jnp.cumsum(la, axis=-1)
    bc = bm.reshape(bsz, nc, c, SSM_GROUPS, SSM_STATE)
    cc = cm.reshape(bsz, nc, c, SSM_GROUPS, SSM_STATE)
    lower = jnp.tril(jnp.ones((c, c), dtype=bool))
    lmat = jnp.exp(jnp.where(lower, a_cum[..., :, None] - a_cum[..., None, :], -jnp.inf))
    cb = jnp.einsum('bclgn,bcsgn->bgcls', cc, bc)
    y_diag = jnp.einsum('bgjcls,bcsgjp->bclgjp', cb[:, :, None] * lmat, xd)
    decay_s = jnp.exp(a_cum[..., -1:] - a_cum).transpose(0, 3, 4, 1, 2)[..., None]
    states = jnp.einsum('bcsgn,bcsgjp->bcgjpn', bc, xd * decay_s)
    a_last = a_cum[..., -1]
    a_cs = jnp.cumsum(a_last, axis=-1)
    a_excl = a_cs - a_last
    before = jnp.tril(jnp.ones((nc, nc), dtype=bool), k=-1)
    w = jnp.exp(jnp.where(before, a_excl[..., :, None] - a_cs[..., None, :], -jnp.inf))
    h_in = jnp.einsum('bgjzc,bcgjpn->bzgjpn', w, states)
    y_off = (jnp.einsum('bzlgn,bzgjpn->bzlgjp', cc, h_in)
             * jnp.exp(a_cum).transpose(0, 3, 4, 1, 2)[..., None])
    return (y_diag + y_off).reshape(bsz, seq, SSM_GROUPS, SSM_HPG, SSM_HEAD_DIM)


def mamba2_mixer(u, w_in, conv_w, conv_b, dt_bias, a_log, d_skip, norm_w, w_out):
    bsz, seq, _ = u.shape
    z, xbc, dt_raw = jnp.split(u @ w_in, [D_INNER, D_INNER + CONV_DIM], axis=-1)
    xbc = jax.nn.silu(dwconv_centred(xbc, conv_w, conv_b))
    xs, bm, cm = jnp.split(xbc, [D_INNER, D_INNER + GN], axis=-1)
    xs = xs.reshape(bsz, seq, SSM_GROUPS, SSM_HPG, SSM_HEAD_DIM)
    bm = bm.reshape(bsz, seq, SSM_GROUPS, SSM_STATE)
    cm = cm.reshape(bsz, seq, SSM_GROUPS, SSM_STATE)
    dt = jax.nn.softplus(dt_raw.reshape(bsz, seq, 2, SSM_HEADS) + dt_bias)
    a = -jnp.exp(a_log).reshape(2, SSM_GROUPS, SSM_HPG)
    dt_fw = dt[:, :, 0].reshape(bsz, seq, SSM_GROUPS, SSM_HPG)
    dt_bw = dt[:, :, 1].reshape(bsz, seq, SSM_GROUPS, SSM_HPG)
    y = (ssd_chunked(xs, dt_fw, a[0], bm, cm)
         + flip(ssd_chunked(flip(xs), flip(dt_bw), a[1], flip(bm), flip(cm)))
         + xs * d_skip.reshape(SSM_GROUPS, SSM_HPG, 1))
    y = y.reshape(bsz, seq, D_INNER) * jax.nn.silu(z)
    y = group_rmsnorm(y, norm_w, SSM_GROUPS)
    return y @ w_out


def conv_glu(u, w_in, conv_w, conv_b, w_out):
    gate, val = jnp.split(u @ w_in, 2, axis=-1)
    return (jax.nn.silu(dwconv_centred(gate, conv_w, conv_b)) * val) @ w_out


def setup_inputs(seed: int = 0) -> dict:
    key = jax.random.key(seed)
    ks = jax.random.split(key, 20)

    def nrm(k, shape, scale):
        return scale * jax.random.normal(k, shape, jnp.float32)

    dt = jnp.exp(jax.random.uniform(ks[10], (N_B, 2, SSM_HEADS), jnp.float32)
                 * (math.log(0.1) - math.log(1e-3)) + math.log(1e-3))
    return {
        "x": nrm(ks[0], (BATCH, SEQ, D_MODEL), 1.0),
        "norm1_w": 1.0 + nrm(ks[1], (DEPTH, D_MODEL), 0.02),
        "norm2_w": 1.0 + nrm(ks[2], (DEPTH, D_MODEL), 0.02),
        "a_w_in": nrm(ks[3], (N_A, D_MODEL, 5 * HG_DIM), D_MODEL ** -0.5),
        "a_lb_logits": nrm(ks[4], (DEPTH + 1, HG_DIM), 0.1),
        "a_norm_w": 1.0 + nrm(ks[5], (N_A, HG_HEAD_DIM), 0.02),
        "a_w_out": nrm(ks[6], (N_A, HG_DIM, D_MODEL), HG_DIM ** -0.5),
        "b_w_in": nrm(ks[7], (N_B, D_MODEL, B_PROJ), D_MODEL ** -0.5),
        "b_conv_w": nrm(ks[8], (N_B, SSM_CONV, CONV_DIM), SSM_CONV ** -0.5),
        "b_conv_b": nrm(ks[9], (N_B, CONV_DIM), 0.02),
        "b_dt_bias": dt + jnp.log(-jnp.expm1(-dt)),
        "b_a_log": jnp.log(jax.random.uniform(ks[11], (N_B, 2, SSM_HEADS), jnp.float32, 1.0, 16.0)),
        "b_d_skip": 1.0 + nrm(ks[12], (N_B, SSM_HEADS), 0.02),
        "b_norm_w": 1.0 + nrm(ks[13], (N_B, D_INNER), 0.02),
        "b_w_out": nrm(ks[14], (N_B, D_INNER, D_MODEL), D_INNER ** -0.5),
        "ffn_w_in": nrm(ks[15], (DEPTH, D_MODEL, 2 * D_FF), D_MODEL ** -0.5),
        "ffn_conv_w": nrm(ks[16], (DEPTH, FFN_CONV, D_FF), FFN_CONV ** -0.5),
        "ffn_conv_b": nrm(ks[17], (DEPTH, D_FF), 0.02),
        "ffn_w_out": nrm(ks[18], (DEPTH, D_FF, D_MODEL), D_FF ** -0.5),
        "final_norm_w": 1.0 + nrm(ks[19], (D_MODEL,), 0.02),
    }


def reference(x, norm1_w, norm2_w, a_w_in, a_lb_logits, a_norm_w, a_w_out, b_w_in, b_conv_w, b_conv_b,
              b_dt_bias, b_a_log, b_d_skip, b_norm_w, b_w_out, ffn_w_in, ffn_conv_w, ffn_conv_b, ffn_w_out,
              final_norm_w):
    lower_bounds = jnp.cumsum(jax.nn.softmax(a_lb_logits.astype(jnp.float32), axis=0), axis=0)
    h = x
    for i in range(DEPTH):
        u = rmsnorm(h, norm1_w[i])
        j = i // N_MIXERS
        if i % N_MIXERS == 0:
            h = h + hgrn2_mixer(u, a_w_in[j], lower_bounds[i], a_norm_w[j], a_w_out[j])
        else:
            h = h + mamba2_mixer(u, b_w_in[j], b_conv_w[j], b_conv_b[j], b_dt_bias[j], b_a_log[j],
                                 b_d_skip[j], b_norm_w[j], b_w_out[j])
        h = h + conv_glu(rmsnorm(h, norm2_w[i]), ffn_w_in[i], ffn_conv_w[i], ffn_conv_b[i], ffn_w_out[i])
    return rmsnorm(h, final_norm_w)
```

```python
import numpy as np
import concourse.bass as bass
import concourse.mybir as mybir
from concourse.bass_utils import run_bass_kernel_spmd
from contextlib import ExitStack

F32 = mybir.dt.float32
BF16 = mybir.dt.bfloat16
ALU = mybir.AluOpType
AF = mybir.ActivationFunctionType
AX = mybir.AxisListType

L = 4096
D = 1024
DFF = 2816
NFC = 22
EPS = 1e-6
SB_LO = 16512
SB_HI = 229344

NW0, LBL, ANW, FCW, FCB, BCW, BCB, NSP = 0, 40, 64, 65, 197, 241, 401, 433
DTB, ALOG, DSK, BNW, NROW = 0, 64, 128, 160, 2208


class Slot:
    __slots__ = ("sem", "val", "busy")

    def __init__(self, sem):
        self.sem = sem
        self.val = 0
        self.busy = False


class Res:
    __slots__ = ("name", "w", "r", "slot")

    def __init__(self, name):
        self.name = name
        self.w = None
        self.r = {}
        self.slot = None


class Prog:
    ENG = ("pe", "act", "dve", "pool", "sp")

    def __init__(self, nc, nslots=88):
        self.nc = nc
        self.st = ExitStack()
        self.q = {e: [] for e in self.ENG}
        self.cnt = {e: 0 for e in self.ENG}
        self.waited = {e: {} for e in self.ENG}
        self.esem = {e: self.st.enter_context(nc.semaphore("s_" + e)) for e in self.ENG}
        self.slots = [Slot(self.st.enter_context(nc.semaphore("d%d" % i))) for i in range(nslots)]
        self.phase_res = []
        self.nname = 0
        self.sb_ptr = SB_LO
        self.psum = nc.alloc_psum_tensor("psum_all", [128, 8, 512], F32)

    def sb(self, name, shape, dt=F32):
        nbytes = int(np.prod(shape[1:])) * (4 if dt == F32 else 2)
        off = (self.sb_ptr + 63) // 64 * 64
        assert off + nbytes <= SB_HI, "SBUF overflow %s: %d" % (name, off + nbytes - SB_HI)
        self.sb_ptr = off + nbytes
        self.nname += 1
        return self.nc.alloc_sbuf_tensor_at("%s_%d" % (name, self.nname), list(shape), dt, offset=off)

    def bank(self, i):
        return self.psum[:, i, :]

    def res(self, name="r", persistent=False):
        r = Res(name)
        if not persistent:
            self.phase_res.append(r)
        return r

    def semh(self, key):
        return self.esem[key] if isinstance(key, str) else key.sem

    def _slot(self, r):
        if r.slot is None:
            for s in self.slots:
                if not s.busy:
                    s.busy = True
                    r.slot = s
                    break
            else:
                raise RuntimeError("out of DMA semaphore slots")
        return r.slot

    def _deps(self, eng, reads, writes):
        deps = {}
        for r in reads:
            if r.w is not None:
                k, v = r.w
                if deps.get(k, 0) < v:
                    deps[k] = v
        for w in writes:
            if w.w is not None:
                k, v = w.w
                if k != eng and deps.get(k, 0) < v:
                    deps[k] = v
            for k, v in w.r.items():
                if k != eng and deps.get(k, 0) < v:
                    deps[k] = v
        return deps

    def _waits(self, eng, deps, skip_self=False):
        wt = self.waited[eng]
        for k, v in deps.items():
            if skip_self and k == eng:
                continue
            if wt.get(k, 0) >= v:
                continue
            wt[k] = v
            self.q[eng].append(("w", k, v))

    def _mark(self, key, t, reads, writes):
        for r in reads:
            if r.r.get(key, 0) < t:
                r.r[key] = t
        for w in writes:
            w.w = (key, t)
            w.r = {}

    def op(self, eng, fn, reads=(), writes=()):
        self._waits(eng, self._deps(eng, reads, writes), skip_self=(eng == "pe"))
        self.cnt[eng] += 1
        t = self.cnt[eng]
        self.q[eng].append(("i", fn, eng))
        self._mark(eng, t, reads, writes)

    def pe(self, fns, reads=(), writes=()):
        self._waits("pe", self._deps("pe", reads, writes), skip_self=True)
        for fn in fns[:-1]:
            self.q["pe"].append(("i", fn, None))
        self.cnt["pe"] += 1
        t = self.cnt["pe"]
        self.q["pe"].append(("i", fns[-1], "pe"))
        self._mark("pe", t, reads, writes)

    def dma(self, queue, out_ap, in_ap, reads, writes, sem_res=None):
        sr = sem_res or writes[0]
        slot = self._slot(sr)
        self._waits(queue, self._deps(queue, reads, writes))
        slot.val += 16
        self.q[queue].append(("d", out_ap, in_ap, slot))
        self._mark(slot, slot.val, reads, writes)

    def dma_multi(self, queue, pairs, reads, writes):
        slot = self._slot(writes[0])
        self._waits(queue, self._deps(queue, reads, writes))
        for (o, i) in pairs:
            slot.val += 16
            self.q[queue].append(("d", o, i, slot))
        self._mark(slot, slot.val, reads, writes)

    def wait_res(self, eng, ress):
        deps = {}
        for r in ress:
            if r.w is not None:
                k, v = r.w
                if deps.get(k, 0) < v:
                    deps[k] = v
        self._waits(eng, deps)

    def barrier(self):
        deps = {e: self.cnt[e] for e in self.ENG if self.cnt[e] > 0}
        for s in self.slots:
            if s.val > 0:
                deps[s] = s.val
        for e in self.ENG:
            self._waits(e, dict(deps))

    def new_phase(self, base):
        self.barrier()
        for r in self.phase_res:
            if r.slot is not None:
                r.slot.busy = False
                r.slot = None
        self.phase_res = []
        self.sb_ptr = base

    def emit(self):
        nc = self.nc
        q = self.q
        semh = self.semh

        def run(eh, items):
            for it in items:
                if it[0] == "w":
                    eh.wait_ge(semh(it[1]), it[2])
                elif it[0] == "i":
                    ins = it[1](eh)
                    if it[2] is not None:
                        ins.then_inc(self.esem[it[2]], 1)
                else:
                    eh.dma_start(out=it[1], in_=it[2]).then_inc(it[3].sem, 16)

        with nc.Block() as block:
            @block.sync
            def _(e):
                run(e, q["sp"])

            @block.scalar
            def _(e):
                run(e, q["act"])

            @block.vector
            def _(e):
                run(e, q["dve"])

            @block.gpsimd
            def _(e):
                run(e, q["pool"])

            @block.tensor
            def _(e):
                run(e, q["pe"])
        self.st.close()


def MM(out, lhsT, rhs, start=True, stop=True):
    return lambda e: e.matmul(out, lhsT=lhsT, rhs=rhs, start=start, stop=stop)


def TR(out, in_, ident):
    return lambda e: e.transpose(out=out, in_=in_, identity=ident)


def ACTF(out, in_, func, **kw):
    return lambda e: e.activation(out=out, in_=in_, func=func, **kw)


def TT(out, in0, in1, op):
    return lambda e: e.tensor_tensor(out=out, in0=in0, in1=in1, op=op)


def TS(out, in0, s1, s2, op0, op1=None):
    if op1 is None:
        return lambda e: e.tensor_scalar(out=out, in0=in0, scalar1=s1, scalar2=None, op0=op0)
    return lambda e: e.tensor_scalar(out=out, in0=in0, scalar1=s1, scalar2=s2, op0=op0, op1=op1)


def STT(out, in0, scalar, in1, op0, op1):
    return lambda e: e.scalar_tensor_tensor(out=out, in0=in0, scalar=scalar, in1=in1, op0=op0, op1=op1)


def CP(out, in_):
    return lambda e: e.tensor_copy(out=out, in_=in_)


def MS(ap, v):
    return lambda e: e.memset(ap, v)


class Ctx:
    pass


def setup_consts(P, C):
    nc = P.nc
    C.ident_f = P.sb("identf", [128, 128], F32)
    C.ident_b = P.sb("identb", [128, 128], BF16)
    C.ones_f = P.sb("onesf", [128, 128], F32)
    C.spt = P.sb("spt", [128, NSP], F32)
    C.rc = P.res("consts", True)
    C.rsp = P.res("sp", True)
    P.dma("sp", C.spt[:], C.sp_d, [], [C.rsp])
    P.op("pool", MS(C.ones_f[:], 1.0), [], [C.rc])
    P.op("pool", MS(C.ident_f[:], 0.0), [], [C.rc])
    P.op("pool", lambda e: e.affine_select(out=C.ident_f[:], in_=C.ident_f[:], pattern=[[-1, 128]],
                                           compare_op=ALU.not_equal, fill=1.0, base=0, channel_multiplier=1),
         [C.rc], [C.rc])
    P.op("pool", CP(C.ident_b[:], C.ident_f[:]), [C.rc], [C.rc])
    C.wout = P.sb("wout", [128, NFC, 1024], BF16)
    C.rwout = P.res("wout", True)
    C.base0 = P.sb_ptr


def load_wout_chunk(P, C, w_d, c, wst, rwst):
    P.dma("sp", wst[:], w_d[c * 128:(c + 1) * 128, :], [], [rwst])
    P.op("pool", CP(C.wout[:, c, :], wst[:]), [rwst], [C.rwout])


def norm_from_tile(P, C, src, rsrc, W, nwcol, T, emit_out):
    sq, rsq = T["sq"], T["rsq"]
    ss, rss = T["ss"], T["rss"]
    rstd, rrstd = T["rstd"], T["rrstd"]
    nb, rnb = T["nbank"], T["rnbank"]
    P.op("act", ACTF(sq[:, :, 0:W], src, AF.Square), [rsrc], [rsq])
    P.op("dve", lambda e: e.tensor_reduce(out=ss[:, 0:W], in_=sq[:, :, 0:W].rearrange("p k w -> p w k"), axis=AX.X,
                                          op=ALU.add), [rsq], [rss])
    P.pe([MM(nb[:, 0:W], C.ones_f[:], ss[:, 0:W])], [rss, C.rc], [rnb])
    P.op("act", ACTF(rstd[:, 0:W], nb[:, 0:W], AF.Ln, scale=1.0 / D, bias=C.eps_col[:, 0:1]), [rnb, C.rc], [rrstd])
    P.op("act", ACTF(rstd[:, 0:W], rstd[:, 0:W], AF.Exp, scale=-0.5), [rrstd], [rrstd])
    for kc in range(8):
        emit_out(kc, rstd[:, 0:W], [rrstd])


def alloc_norm_tmps(P, W):
    T = {}
    T["sq"] = P.sb("sq", [128, 8, W], F32); T["rsq"] = P.res("sq")
    T["ss"] = P.sb("ss", [128, W], F32); T["rss"] = P.res("ss")
    T["rstd"] = P.sb("rstd", [128, W], F32); T["rrstd"] = P.res("rstd")
    T["nbank"] = P.bank(7); T["rnbank"] = P.res("nbank")
    return T


def phase_init_norm(P, C, nidx):
    P.new_phase(C.base0)
    W = 512
    xt = [P.sb("xt", [128, 8, W], F32) for _ in range(2)]
    rxt = [P.res("xt") for _ in range(2)]
    ut = [P.sb("ut", [128, 8, W], BF16) for _ in range(2)]
    rut = [P.res("ut") for _ in range(2)]
    T = alloc_norm_tmps(P, W)
    for j in range(L // W):
        b = j % 2
        sl = slice(j * W, (j + 1) * W)
        P.dma("sp", xt[b][:], C.xT_d[:, :, sl], [], [rxt[b]])
        P.dma("act", C.hT_d[:, :, sl], xt[b][:], [rxt[b]], [C.rh[j]])

        def out(kc, rstd, rd, b=b):
            P.op("dve", STT(ut[b][:, kc, :], xt[b][:, kc, :], C.spt[:, NW0 + nidx * 8 + kc:NW0 + nidx * 8 + kc + 1], rstd,
                            ALU.mult, ALU.mult), [rxt[b], C.rsp] + rd, [rut[b]])
        norm_from_tile(P, C, xt[b][:], rxt[b], W, None, T, out)
        P.dma("act", C.uT_d[:, :, sl], ut[b][:], [rut[b]], [C.ru[j]])


def load_uT(P, C):
    uT = P.sb("uT", [128, 8, L], BF16)
    ruT = [P.res("uT%d" % j) for j in range(8)]
    for j in range(8):
        sl = slice(j * 512, (j + 1) * 512)
        P.dma("sp", uT[:, :, sl], C.uT_d[:, :, sl], [C.ru[j]], [ruT[j]])
    return uT, ruT


def tail_stages(P, C, yT, ry, K, W, t0, TB, i, nidx, final):
    b = i % 2
    ht, rht = TB["ht"][b], TB["rht"][b]
    ut, rut = TB["ut"][b], TB["rut"][b]
    T = TB["T"]
    jt = t0 // 512
    nbk = len(TB["banks"])
    sq, rsq, ss, rss, rstd, rrstd, nb, rnb = T["sq"], T["rsq"], T["ss"], T["rss"], T["rstd"], T["rrstd"], T["nbank"], T["rnbank"]

    def s_load():
        P.dma("sp", ht[:, :, 0:W], C.hT_d[:, :, t0:t0 + W], [C.rh[jt]], [rht])

    def s_dc(dc):
        bk = P.bank(TB["banks"][dc % nbk])
        rbk = TB["rbk"][dc % nbk]
        P.pe([MM(bk[:, 0:W], C.wout[:, c, dc * 128:(dc + 1) * 128], yT[:, c, :], c == 0, c == K - 1) for c in range(K)],
             [C.rwout] + ry, [rbk])
        P.op("dve", TT(ht[:, dc, 0:W], bk[:, 0:W], ht[:, dc, 0:W], ALU.add), [rbk, rht], [rht])

    def s_store_sq():
        if not final:
            P.dma("act", C.hT_d[:, :, t0:t0 + W], ht[:, :, 0:W], [rht], [C.rh[jt]])
        P.op("act", ACTF(sq[:, :, 0:W], ht[:, :, 0:W], AF.Square), [rht], [rsq])

    def s_red():
        if TB.get("red_pool"):
            P.op("pool", TT(ss[:, 0:W], sq[:, 0, 0:W], sq[:, 1, 0:W], ALU.add), [rsq], [rss])
            for kc in range(2, 8):
                P.op("pool", TT(ss[:, 0:W], ss[:, 0:W], sq[:, kc, 0:W], ALU.add), [rsq, rss], [rss])
        else:
            P.op("dve", lambda e: e.tensor_reduce(out=ss[:, 0:W], in_=sq[:, :, 0:W].rearrange("p k w -> p w k"), axis=AX.X,
                                                  op=ALU.add), [rsq], [rss])

    def s_red_b():
        P.pe([MM(nb[:, 0:W], C.ones_f[:], ss[:, 0:W])], [rss, C.rc], [rnb])

    def s_act():
        P.op("act", ACTF(rstd[:, 0:W], nb[:, 0:W], AF.Ln, scale=1.0 / D, bias=C.eps_col[:, 0:1]), [rnb, C.rc], [rrstd])
        P.op("act", ACTF(rstd[:, 0:W], rstd[:, 0:W], AF.Exp, scale=-0.5), [rrstd], [rrstd])

    def s_out(k0, k1):
        for kc in range(k0, k1):
            col = NW0 + nidx * 8 + kc
            P.op("dve", STT(ut[:, kc, 0:W], ht[:, kc, 0:W], C.spt[:, col:col + 1], rstd[:, 0:W], ALU.mult, ALU.mult),
                 [rht, C.rsp, rrstd], [rut])

    def s_fin():
        if final:
            P.dma("act", C.outT_d[:, :, t0:t0 + W], ut[:, :, 0:W], [rut], [C.rout[jt]])
        else:
            P.dma("act", C.uT_d[:, :, t0:t0 + W], ut[:, :, 0:W], [rut], [C.ru[jt]])

    st = [lambda: (s_load(), s_dc(0))]
    for dc in range(1, 8):
        st.append(lambda dc=dc: s_dc(dc))
    if TB.get("split_norm"):
        st += [s_store_sq, s_red, s_red_b, s_act, lambda: s_out(0, 4), lambda: (s_out(4, 8), s_fin())]
    else:
        st += [s_store_sq, lambda: (s_red(), s_red_b(), s_act()), lambda: s_out(0, 4), lambda: (s_out(4, 8), s_fin())]
    return st


def tail_tile(P, C, yT, ry, K, W, t0, TB, i, nidx, final):
    for f in tail_stages(P, C, yT, ry, K, W, t0, TB, i, nidx, final):
        f()


def alloc_tail(P, W, final, banks=(5, 6)):
    TB = {}
    TB["banks"] = banks
    TB["ht"] = [P.sb("ht", [128, 8, W], F32) for _ in range(2)]
    TB["rht"] = [P.res("ht") for _ in range(2)]
    TB["rbk"] = [P.res("tbk") for _ in range(2)]
    TB["ut"] = [P.sb("utl", [128, 8, W], F32 if final else BF16) for _ in range(2)]
    TB["rut"] = [P.res("utl") for _ in range(2)]
    TB["T"] = alloc_norm_tmps(P, W)
    return TB


def phase_ffn(P, C, layer, nidx_next, final):
    P.new_phase(C.base0)
    uT, ruT = load_uT(P, C)
    wst = [P.sb("wst", [128, 1024], F32) for _ in range(4)]
    rwst = [P.res("wst") for _ in range(4)]
    wbf = [P.sb("wbf", [128, 2, 8, 128], BF16) for _ in range(2)]
    rwbf = [P.res("wbf") for _ in range(2)]
    grow = [P.sb("grow", [128, L + 4], BF16) for _ in range(2)]
    rg = [[P.res("g") for _ in range(8)] for _ in range(2)]
    rgpad = P.res("gpad")
    dg = [P.sb("dg", [128, 3, 128], BF16) for _ in range(2)]
    rdg = [P.res("dg") for _ in range(2)]
    sg = [P.sb("sg", [128, 512], F32) for _ in range(2)]
    rsg = [P.res("sg") for _ in range(2)]
    yrow = [P.sb("yrow", [128, L], BF16) for _ in range(2)]
    ryrow = [P.res("yrow") for _ in range(2)]
    rb = [P.res("fbk%d" % i) for i in range(6)]
    for b in range(2):
        P.op("pool", MS(grow[b][:, 0:2], 0.0), [], [rgpad])
        P.op("pool", MS(grow[b][:, L + 2:L + 4], 0.0), [], [rgpad])
    wi = 0
    for c in range(NFC):
        b = c % 2
        for h2 in range(2):
            s = wi % 4
            wi += 1
            P.dma("sp", wst[s][:], C.f_win_d[layer, h2 * NFC + c].rearrange("p k f -> p (k f)"), [], [rwst[s]])
            P.op("pool", CP(wbf[b][:, h2].rearrange("p k f -> p (k f)"), wst[s][:]), [rwst[s]], [rwbf[b]])
        for k in range(3):
            col = FCW + (layer * 3 + k) * NFC + c
            P.op("dve", TS(dg[b][:, k, :], C.ident_b[:], C.spt[:, col:col + 1], None, ALU.mult), [C.rc, C.rsp], [rdg[b]])
        for j in range(8):
            bk = P.bank(j % 2)
            P.pe([MM(bk, wbf[b][:, 0, kc, :], uT[:, kc, j * 512:(j + 1) * 512], kc == 0, kc == 7) for kc in range(8)],
                 [rwbf[b], ruT[j]], [rb[j % 2]])
            P.op("dve", CP(grow[b][:, 2 + j * 512:2 + (j + 1) * 512], bk), [rb[j % 2]], [rg[b][j]])
        for j in range(8):
            bv = P.bank(2 + j % 2)
            P.pe([MM(bv, wbf[b][:, 1, kc, :], uT[:, kc, j * 512:(j + 1) * 512], kc == 0, kc == 7) for kc in range(8)],
                 [rwbf[b], ruT[j]], [rb[2 + j % 2]])
            bc = P.bank(4 + j % 2)
            nb = [rg[b][jj] for jj in (j - 1, j, j + 1) if 0 <= jj < 8] + [rgpad, rdg[b]]
            P.pe([MM(bc, dg[b][:, k, :], grow[b][:, 1 + k + j * 512:1 + k + (j + 1) * 512], k == 0, k == 2) for k in range(3)],
                 nb, [rb[4 + j % 2]])
            colb = FCB + layer * NFC + c
            P.op("act", ACTF(sg[j % 2][:], bc, AF.Silu, bias=C.spt[:, colb:colb + 1]), [rb[4 + j % 2], C.rsp], [rsg[j % 2]])
            P.op("dve", TT(yrow[b][:, j * 512:(j + 1) * 512], sg[j % 2][:], bv, ALU.mult), [rsg[j % 2], rb[2 + j % 2]],
                 [ryrow[b]])
        P.dma("act", C.y_d[c], yrow[b][:], [ryrow[b]], [C.ry[c]])
        load_wout_chunk(P, C, C.f_wout_d[layer], c, wst[wi % 4], rwst[wi % 4])
        wi += 1
    P.new_phase(C.base0)
    W = 512
    yt = [P.sb("yt", [128, NFC, W], BF16) for _ in range(2)]
    ryt = [P.res("yt") for _ in range(2)]
    TB = alloc_tail(P, W, final)
    TB["red_pool"] = True
    TB["split_norm"] = True
    prev_norm = []
    for i in range(L // W):
        b = i % 2
        t0 = i * W
        P.dma_multi("sp", [(yt[b][:, c0:c1, :], C.y_d[c0:c1, :, t0:t0 + W].rearrange("c p t -> p c t"))
                           for (c0, c1) in ((0, 8), (8, 15), (15, NFC))], list(C.ry), [ryt[b]])
        stg = tail_stages(P, C, yt[b][:], [ryt[b]], NFC, W, t0, TB, i, nidx_next, final)
        dcs, norms = stg[:8], stg[8:]
        slot_of = {0: 0, 1: 1, 3: 2, 4: 3, 5: 4, 6: 5}
        for k in range(8):
            dcs[k]()
            if prev_norm and k in slot_of:
                prev_norm[slot_of[k]]()
        prev_norm = norms
    for f in prev_norm:
        f()


HK = 8


def phase_hgrn(P, C, nidx_next):
    P.new_phase(C.base0)
    eblt = [P.sb("ebl", [128, 8, 64], F32) for _ in range(2)]
    rebl = [P.res("ebl") for _ in range(2)]
    baseA2 = P.sb_ptr
    uT, ruT = load_uT(P, C)
    lbt = P.sb("lbt", [128, 6, 8], F32)
    rlb = P.res("lb")
    lg = C.spt[:, LBL:LBL + 24].rearrange("p (r h) -> p r h", r=3)
    P.op("dve", TT(lbt[:, 3, :], lg[:, 0, :], lg[:, 1, :], ALU.max), [C.rsp], [rlb])
    P.op("dve", TT(lbt[:, 3, :], lbt[:, 3, :], lg[:, 2, :], ALU.max), [C.rsp, rlb], [rlb])
    et = P.sb("et", [128, 3, 8], F32)
    for r in range(3):
        P.op("dve", TT(et[:, r, :], lg[:, r, :], lbt[:, 3, :], ALU.subtract), [C.rsp, rlb], [rlb])
    P.op("act", ACTF(et[:], et[:], AF.Exp), [rlb], [rlb])
    P.op("dve", TT(lbt[:, 4, :], et[:, 0, :], et[:, 1, :], ALU.add), [rlb], [rlb])
    P.op("dve", TT(lbt[:, 4, :], lbt[:, 4, :], et[:, 2, :], ALU.add), [rlb], [rlb])
    P.op("dve", lambda e: e.reciprocal(out=lbt[:, 5, :], in_=lbt[:, 4, :]), [rlb], [rlb])
    P.op("dve", TT(lbt[:, 0, :], et[:, 0, :], lbt[:, 5, :], ALU.mult), [rlb], [rlb])
    P.op("dve", TS(lbt[:, 1, :], lbt[:, 0, :], -1.0, 1.0, ALU.mult, ALU.add), [rlb], [rlb])
    P.op("dve", TS(lbt[:, 2, :], lbt[:, 1, :], -1.0, None, ALU.mult), [rlb], [rlb])

    wst = [P.sb("wst", [128, 1024], F32) for _ in range(2)]
    rwst = [P.res("wst") for _ in range(2)]
    wbf = [P.sb("wbf", [128, 5, 8, 128], BF16) for _ in range(2)]
    rwbf = [P.res("wbf") for _ in range(2)]
    W = 512
    mk01 = [P.sb("mk01", [128, W], F32) for _ in range(2)]; rmk01 = P.res("mk01")
    qs = [P.sb("qs", [128, W], F32) for _ in range(3)]; rqs = [P.res("qs") for _ in range(3)]
    gsg = [P.sb("gsg", [128, W], F32) for _ in range(1)]; rgsg = [P.res("gsg") for _ in range(1)]

    def mk(name):
        return ([[P.sb(name, [128, W], F32) for _ in range(2)] for _ in range(2)],
                [[P.res(name) for _ in range(2)] for _ in range(2)])
    sgt = [[P.sb("sgt", [128, W], F32) for _ in range(3)] for _ in range(2)]
    rsgt = [[P.res("sgt") for _ in range(3)] for _ in range(2)]
    ft, rft = mk("ft")
    bt, rbt = mk("bt")
    ebt, rebt = mk("ebt")
    ob = [P.sb("ob", [128, 6, W], BF16) for _ in range(2)]; rob = [P.res("ob") for _ in range(2)]
    obx = [P.sb("obx", [128, 2, W], BF16) for _ in range(2)]; robx = [P.res("obx") for _ in range(2)]
    rbk = [P.res("abk%d" % i) for i in range(8)]
    for d in range(2):
        P.op("pool", MS(mk01[d][:], 1.0), [], [rmk01])
        pos = 0 if d == 0 else 63
        P.op("pool", MS(mk01[d][:].rearrange("p (c t) -> p c t", t=64)[:, :, pos:pos + 1], 0.0), [rmk01], [rmk01])
    wi = [0]
    tiles = [(hd, j) for hd in range(8) for j in range(8)]

    def load_w(hd):
        hb = hd % 2
        for i in range(5):
            s = wi[0] % 2
            wi[0] += 1
            P.dma("sp", wst[s][:], C.a_win_d[i * 8 + hd].rearrange("p k f -> p (k f)"), [], [rwst[s]])
            P.op("pool", CP(wbf[hb][:, i].rearrange("p k f -> p (k f)"), wst[s][:]), [rwst[s]], [rwbf[hb]])
        s = wi[0] % 2
        wi[0] += 1
        load_wout_chunk(P, C, C.a_wout_d, hd, wst[s], rwst[s])

    def stageX(ti):
        hd, j = tiles[ti]
        hb = hd % 2
        tb = ti % 2
        t3 = ti % 3
        if j == 0 and hd == 0:
            load_w(0)
        if j == 1 and hd + 1 < 8:
            load_w(hd + 1)
        bks = []
        for i in range(5):
            bi = i if (i >= 3 or tb == 0) else 5 + i
            bks.append(bi)
            P.pe([MM(P.bank(bi), wbf[hb][:, i, kc, :], uT[:, kc, j * W:(j + 1) * W], kc == 0, kc == 7) for kc in range(8)],
                 [rwbf[hb], ruT[j]], [rbk[bi]])
        o, ro = obx[tb], robx[tb]
        P.op("act", ACTF(qs[t3][:], P.bank(bks[0]), AF.Sigmoid), [rbk[bks[0]]], [rqs[t3]])
        P.op("act", ACTF(gsg[0][:], P.bank(bks[4]), AF.Sigmoid), [rbk[bks[4]]], [rgsg[0]])
        for d in range(2):
            P.op("act", ACTF(sgt[d][t3][:], P.bank(bks[1 + d]), AF.Sigmoid), [rbk[bks[1 + d]]], [rsgt[d][t3]])
        P.op("act", ACTF(o[:, 0, :], P.bank(bks[3]), AF.Copy), [rbk[bks[3]]], [ro])
        P.op("dve", TT(qs[t3][:], P.bank(bks[0]), qs[t3][:], ALU.mult), [rbk[bks[0]], rqs[t3]], [rqs[t3]])
        P.op("dve", TT(o[:, 1, :], P.bank(bks[4]), gsg[0][:], ALU.mult), [rbk[bks[4]], rgsg[0]], [ro])
        for s2 in range(2):
            dst = C.rows_d[6:8, 2 * j + s2, :, hd, :].rearrange("k p t -> p k t")
            P.dma("sp", dst, o[:, :, s2 * 256:(s2 + 1) * 256], [ro], [C.rrows[2 * j + s2]])
        for d in range(2):
            sg = sgt[d][t3]
            P.op("pool", TS(ft[d][tb][:], sg[:], lbt[:, 1, hd:hd + 1], lbt[:, 0, hd:hd + 1], ALU.mult, ALU.add),
                 [rsgt[d][t3], rlb], [rft[d][tb]])
            P.op("pool", TS(sg[:], sg[:], lbt[:, 2, hd:hd + 1], lbt[:, 1, hd:hd + 1], ALU.mult, ALU.add),
                 [rsgt[d][t3], rlb], [rsgt[d][t3]])

    def stageY(ti):
        hd, j = tiles[ti]
        tb = ti % 2
        t3 = ti % 3
        o, ro = ob[tb], rob[tb]
        for d in range(2):
            P.op("act", ACTF(ft[d][tb][:], ft[d][tb][:], AF.Ln), [rft[d][tb]], [rft[d][tb]])
        for d in range(2):
            f_, b_ = ft[d][tb], bt[d][tb]
            if d == 0:
                P.op("dve", lambda e, f_=f_, b_=b_: e.tensor_tensor_scan(out=b_[:], data0=mk01[0][:], data1=f_[:], initial=0.0,
                                                                       op0=ALU.mult, op1=ALU.add),
                     [rft[d][tb], rmk01], [rbt[d][tb]])
            else:
                P.op("dve", lambda e, f_=f_, b_=b_: e.tensor_tensor_scan(out=b_[:, ::-1], data0=mk01[1][:, ::-1],
                                                                       data1=f_[:, ::-1], initial=0.0,
                                                                       op0=ALU.mult, op1=ALU.add),
                     [rft[d][tb], rmk01], [rbt[d][tb]])
        for d in range(2):
            P.op("act", ACTF(ebt[d][tb][:], bt[d][tb][:], AF.Exp), [rbt[d][tb]], [rebt[d][tb]])
            P.op("act", ACTF(bt[d][tb][:], bt[d][tb][:], AF.Exp, scale=-1.0), [rbt[d][tb]], [rbt[d][tb]])
        for d in range(2):
            eb_, reb_ = ebt[d][tb], rebt[d][tb]
            lastpos = 63 if d == 0 else 0
            ebv = eb_[:].rearrange("p (c t) -> p c t", t=64)
            P.op("pool", CP(eblt[d][:, hd, j * 8:(j + 1) * 8].unsqueeze(2), ebv[:, :, lastpos:lastpos + 1]),
                 [reb_], [rebl[d]])
            P.op("pool", TT(o[:, 3 * d + 0, :], qs[t3][:], eb_[:], ALU.mult), [rqs[t3], reb_], [ro])
            P.op("dve", TT(o[:, 3 * d + 1, :], sgt[d][t3][:], bt[d][tb][:], ALU.mult), [rsgt[d][t3], rbt[d][tb]], [ro])
            P.op("dve", TT(o[:, 3 * d + 2, :].rearrange("p (c t) -> p c t", t=64),
                           o[:, 3 * d + 1, :].rearrange("p (c t) -> p c t", t=64),
                           ebv[:, :, lastpos:lastpos + 1].to_broadcast([128, 8, 64]), ALU.mult), [ro, reb_], [ro])
        for s2 in range(2):
            dst = C.rows_d[0:6, 2 * j + s2, :, hd, :].rearrange("k p t -> p k t")
            P.dma("sp", dst, o[:, :, s2 * 256:(s2 + 1) * 256], [ro], [C.robwd[2 * j + s2]])

    stageX(0)
    for ti in range(len(tiles)):
        if ti + 1 < len(tiles):
            stageX(ti + 1)
        stageY(ti)

    P.new_phase(baseA2)
    WS = 256
    maskf = P.sb("maskf", [64, 64], F32)
    maskb = P.sb("maskb", [64, 64], F32)
    rmask = P.res("mask")
    P.op("pool", MS(maskf[:], 1.0), [], [rmask])
    P.op("pool", MS(maskb[:], 1.0), [], [rmask])
    P.op("pool", lambda e: e.affine_select(out=maskf[:], in_=maskf[:], pattern=[[1, 64]], compare_op=ALU.is_ge, fill=0.0,
                                           base=0, channel_multiplier=-1), [rmask], [rmask])
    P.op("pool", lambda e: e.affine_select(out=maskb[:], in_=maskb[:], pattern=[[-1, 64]], compare_op=ALU.is_ge, fill=0.0,
                                           base=0, channel_multiplier=1), [rmask], [rmask])
    S32 = P.sb("S32", [128, 8, 128], F32); rS32 = P.res("S32")
    Sbf = P.sb("Sbf", [128, 8, 128], BF16); rSbf = P.res("Sbf")
    Stmp = P.sb("Stmp", [128, 4, 128], F32); rStmp = P.res("Stmp")
    rt = [P.sb("rt", [128, 4, 8, WS], BF16) for _ in range(2)]
    rrt = [P.res("rt") for _ in range(2)]
    kdtm = [P.sb("kdtm", [64, 8, 128], BF16) for _ in range(2)]; rkdtm = [P.res("kdtm") for _ in range(2)]
    vtm = [P.sb("vtm", [64, 8, 128], BF16) for _ in range(2)]; rvtm = [P.res("vtm") for _ in range(2)]
    attm = [P.sb("attm", [64, 8, 64], BF16) for _ in range(2)]; rattm = [P.res("attm") for _ in range(2)]
    obt = [P.sb("obt", [128, 8, WS], F32) for _ in range(2)]; robt = [P.res("obt") for _ in range(2)]
    obl = [P.sb("obl", [128, 8, WS], F32) for _ in range(2)]; robl = [P.res("obl") for _ in range(2)]
    rst = P.sb("rst", [128, 8, WS], F32); rrst = P.res("rst")
    sqh = P.sb("sqh", [128, 8, WS], F32); rsqh = P.res("sqh")
    yT = [P.sb("yT", [128, 8, WS], BF16) for _ in range(2)]; ryT = [P.res("yT") for _ in range(2)]
    gsb = [P.sb("gsb", [128, 8, WS], BF16) for _ in range(2)]; rgsb = [P.res("gsb") for _ in range(2)]
    TB = alloc_tail(P, WS, False, banks=(6,))
    rTK, rTV, rAT, rO, rUA, rUB = [P.res("a2bk%d" % i) for i in range(6)]
    bTK = P.bank(0).bitcast(BF16)
    bTV = P.bank(1).bitcast(BF16)
    bAT, bO, bU = P.bank(2), P.bank(3), [P.bank(4), P.bank(5)]
    rU = [rUA, rUB]

    for ps in range(2):
        fwd = ps == 1
        d = 0 if fwd else 1
        mask = maskf if fwd else maskb
        P.op("pool", MS(S32[:], 0.0), [], [rS32])
        P.op("pool", MS(Sbf[:], 0.0), [], [rSbf])
        chunks = []
        sts = list(range(16)) if fwd else list(range(15, -1, -1))
        for si, st in enumerate(sts):
            ccs = list(range(4)) if fwd else list(range(3, -1, -1))
            for cc in ccs:
                chunks.append((si, st, cc))
        NCH = len(chunks)

        def load_st(si, st):
            b = si % 2
            if fwd:
                P.dma("sp", rt[b][:, 0:3].rearrange("p k h t -> p k (h t)"),
                      C.rows_d[0:3, st].rearrange("k p h t -> p k (h t)"), [C.rrows[st]], [rrt[b]])
                P.dma("sp", rt[b][:, 3:4].rearrange("p k h t -> p k (h t)"),
                      C.rows_d[6:7, st].rearrange("k p h t -> p k (h t)"), [C.rrows[st]], [rrt[b]])
                P.dma("sp", obl[b][:], C.obwd_d[st], [C.robwd[st]], [robl[b]])
            else:
                P.dma("sp", rt[b][:, 0:4].rearrange("p k h t -> p k (h t)"),
                      C.rows_d[3:7, st].rearrange("k p h t -> p k (h t)"), [C.rrows[st]], [rrt[b]])

        def front(n):
            si, st, cc = chunks[n]
            b = si % 2
            nb = n % 2
            tc = slice(cc * 64, (cc + 1) * 64)
            R = rt[b]
            P.pe([TR(bTK[0:64, hd * 128:(hd + 1) * 128], R[:, 2, hd, tc], C.ident_b[:]) for hd in range(8)],
                 [rrt[b], C.rc], [rTK])
            P.op("act", ACTF(kdtm[nb][:].rearrange("s h k -> s (h k)"), bTK[0:64, :], AF.Copy), [rTK], [rkdtm[nb]])
            P.pe([TR(bTV[0:64, hd * 128:(hd + 1) * 128], R[:, 3, hd, tc], C.ident_b[:]) for hd in range(8)],
                 [rrt[b], C.rc], [rTV])
            P.op("act", ACTF(vtm[nb][:].rearrange("s h k -> s (h k)"), bTV[0:64, :], AF.Copy), [rTV], [rvtm[nb]])
            P.pe([MM(bAT[0:64, hd * 64:(hd + 1) * 64], R[:, 1, hd, tc], R[:, 0, hd, tc]) for hd in range(8)],
                 [rrt[b]], [rAT])
            P.op("dve", TT(attm[nb][:], bAT[0:64, :].rearrange("s (h t) -> s h t", h=8),
                           mask[:].unsqueeze(1).to_broadcast([64, 8, 64]), ALU.mult), [rAT, rmask], [rattm[nb]])

        def back(n):
            si, st, cc = chunks[n]
            b = si % 2
            nb = n % 2
            tc = slice(cc * 64, (cc + 1) * 64)
            R = rt[b]
            for half in range(2):
                P.pe([MM(bU[half][:, q * 128:(q + 1) * 128], kdtm[nb][:, half * 4 + q, :], vtm[nb][:, half * 4 + q, :])
                      for q in range(4)], [rkdtm[nb], rvtm[nb]], [rU[half]])
            fns = []
            for hd in range(8):
                fns.append(MM(bO[:, hd * 64:(hd + 1) * 64], vtm[nb][:, hd, :], attm[nb][:, hd, :], True, False))
                fns.append(MM(bO[:, hd * 64:(hd + 1) * 64], Sbf[:, hd, :], R[:, 0, hd, tc], False, True))
            P.pe(fns, [rvtm[nb], rattm[nb], rSbf, rrt[b]], [rO])
            gch = st * 4 + cc
            for half in range(2):
                hs = slice(half * 4, half * 4 + 4)
                P.op("dve", TT(Stmp[:], S32[:, hs, :], eblt[d][:, hs, gch:gch + 1].to_broadcast([128, 4, 128]), ALU.mult),
                     [rS32, rebl[d]], [rStmp])
                P.op("dve", TT(S32[:, hs, :], bU[half].rearrange("k (h v) -> k h v", h=4), Stmp[:], ALU.add),
                     [rU[half], rStmp], [rS32])
            P.op("act", ACTF(Sbf[:], S32[:], AF.Copy), [rS32], [rSbf])
            oview = bO.rearrange("v (h t) -> v h t", h=8)
            if fwd:
                P.op("dve", TT(obt[b][:, :, tc], oview, obl[b][:, :, tc], ALU.add), [rO, robl[b]], [robt[b]])
            else:
                P.op("act", ACTF(obt[b][:, :, tc], oview, AF.Copy), [rO], [robt[b]])

        def finish_stages(si, st):
            b = si % 2
            o = obt[b]
            nbk = TB["T"]["nbank"]
            rnbk = TB["T"]["rnbank"]

            def f0():
                P.op("act", ACTF(sqh[:], o[:], AF.Square), [robt[b]], [rsqh])
                P.dma("sp", gsb[b][:], C.rows_d[7, st], [C.rrows[st]], [rgsb[b]])

            def f1(h2):
                P.pe([MM(nbk[:, q * WS:(q + 1) * WS], C.ones_f[:], sqh[:, h2 * 2 + q, :]) for q in range(2)],
                     [rsqh, C.rc], [rnbk])
                P.op("act", ACTF(rst[:, h2 * 2:h2 * 2 + 2, :].rearrange("p h t -> p (h t)"), nbk, AF.Ln, scale=1.0 / 128,
                                 bias=C.eps_col[:, 0:1]), [rnbk, C.rc], [rrst])

            def f5():
                P.op("act", ACTF(rst[:], rst[:], AF.Exp, scale=-0.5), [rrst], [rrst])
                P.op("pool", TT(rst[:], rst[:], gsb[b][:], ALU.mult), [rrst, rgsb[b]], [rrst])

            def f6():
                P.op("dve", STT(yT[b][:], o[:], C.spt[:, ANW:ANW + 1], rst[:], ALU.mult, ALU.mult),
                     [robt[b], rrst, C.rsp], [ryT[b]])

            stg = [f0] + [lambda h2=h2: f1(h2) for h2 in range(4)] + [f5, f6]
            stg += tail_stages(P, C, yT[b][:], [ryT[b]], 8, WS, st * WS, TB, si, nidx_next, False)
            return stg

        pend = []

        def tick():
            while pend and pend[0][0] <= 0:
                it = pend.pop(0)
                it[1]()
            for it in pend:
                it[0] -= 1

        load_st(0, sts[0])
        front(0)
        for n in range(NCH):
            si, st, cc = chunks[n]
            if n % 4 == 0 and si + 1 < 16:
                load_st(si + 1, sts[si + 1])
            if n + 1 < NCH:
                front(n + 1)
            back(n)
            if n % 4 == 3:
                if not fwd:
                    P.dma("act", C.obwd_d[st], obt[si % 2][:], [robt[si % 2]], [C.robwd[st]])
                else:
                    for k, fn in enumerate(finish_stages(si, st)):
                        pend.append([k // 3, fn])
                    pend.sort(key=lambda it: it[0])
            tick()
        while pend:
            pend.pop(0)[1]()


def phase_mamba(P, C, nidx_next, stop=None):
    P.new_phase(C.base0)
    rowp = P.sb("rowp", [128, NROW], F32); rrowp = P.res("rowp")
    dt_all = P.sb("dt_all", [128, 32, 64], F32); rdt = P.res("dt_all")
    aneg = P.sb("aneg", [128, 64], F32)
    Ddiag = P.sb("Ddiag", [128, 32, 128], BF16); rDd = P.res("Ddiag")
    baseM2 = P.sb_ptr
    P.dma("sp", rowp[:], C.rowp_d, [], [rrowp])
    P.op("act", ACTF(aneg[:], rowp[:, ALOG:ALOG + 64], AF.Exp), [rrowp], [rrowp])
    P.op("dve", TS(aneg[:], aneg[:], -1.0, None, ALU.mult), [rrowp], [rrowp])
    uT, ruT = load_uT(P, C)
    wst = [P.sb("wst", [128, 1024], F32) for _ in range(2)]
    rwst = [P.res("wst") for _ in range(2)]
    Wz = P.sb("Wz", [128, 8, 2048], BF16); rWz = P.res("Wz")
    Wdt = P.sb("Wdt", [128, 8, 64], BF16)
    wi = 0
    for h in range(32):
        P.op("pool", TT(Ddiag[:, h, :], C.ident_f[:], rowp[:, DSK + h:DSK + h + 1].to_broadcast([128, 128]), ALU.mult),
             [C.rc, rrowp], [rDd])
    zt = [P.sb("zt", [128, 2048], BF16) for _ in range(1)]; rzt = [P.res("zt") for _ in range(1)]
    rbk = [P.res("mbk%d" % i) for i in range(8)]
    wbf = [P.sb("wbf", [128, 8, 128], BF16) for _ in range(2)]; rwbf = [P.res("wbf") for _ in range(2)]
    prow = [P.sb("prow", [128, L + 8], BF16) for _ in range(2)]
    rpr = [[P.res("pr") for _ in range(8)] for _ in range(2)]
    rppad = P.res("ppad")
    dg = [P.sb("dg5", [128, 5, 128], BF16) for _ in range(2)]; rdg = [P.res("dg5") for _ in range(2)]
    ot = [P.sb("xot", [128, 512], BF16) for _ in range(2)]; rot = [P.res("xot") for _ in range(2)]
    for b in range(2):
        P.op("pool", MS(prow[b][:, 0:4], 0.0), [], [rppad])
        P.op("pool", MS(prow[b][:, L + 4:L + 8], 0.0), [], [rppad])
    oi = 0
    for fc in range(32):
        b = fc % 2
        s = wi % 2
        wi += 1
        P.dma("sp", wst[s][:], C.b_wxbc_d[fc].rearrange("p k f -> p (k f)"), [], [rwst[s]])
        P.op("pool", CP(wbf[b][:].rearrange("p k f -> p (k f)"), wst[s][:]), [rwst[s]], [rwbf[b]])
        if fc < 16:
            s = wi % 2
            wi += 1
            load_wout_chunk(P, C, C.b_wout_d, fc, wst[s], rwst[s])
            kc_, hf_ = fc // 2, fc % 2
            s = wi % 2
            wi += 1
            P.dma("sp", wst[s][:], C.b_wz_d[:, kc_, hf_ * 1024:(hf_ + 1) * 1024], [], [rwst[s]])
            P.op("pool", CP(Wz[:, kc_, hf_ * 1024:(hf_ + 1) * 1024], wst[s][:]), [rwst[s]], [rWz])
        if fc == 16:
            s = wi % 2
            wi += 1
            P.dma("sp", wst[s][:, 0:512], C.b_wdt_d.rearrange("p k f -> p (k f)"), [], [rwst[s]])
            P.op("pool", CP(Wdt[:].rearrange("p k f -> p (k f)"), wst[s][:, 0:512]), [rwst[s]], [rWz])
        for k in range(5):
            col = BCW + k * 32 + fc
            P.op("dve", TS(dg[b][:, k, :], C.ident_b[:], C.spt[:, col:col + 1], None, ALU.mult), [C.rc, C.rsp], [rdg[b]])
        for j in range(8):
            bk = P.bank(3 + j % 2)
            P.pe([MM(bk, wbf[b][:, kc, :], uT[:, kc, j * 512:(j + 1) * 512], kc == 0, kc == 7) for kc in range(8)],
                 [rwbf[b], ruT[j]], [rbk[3 + j % 2]])
            P.op("dve", CP(prow[b][:, 4 + j * 512:4 + (j + 1) * 512], bk), [rbk[3 + j % 2]], [rpr[b][j]])
        for j in range(8):
            bc = P.bank(5 + j % 2)
            nbr = [rpr[b][jj] for jj in (j - 1, j, j + 1) if 0 <= jj < 8] + [rppad, rdg[b]]
            P.pe([MM(bc, dg[b][:, k, :], prow[b][:, 2 + k + j * 512:2 + k + (j + 1) * 512], k == 0, k == 4) for k in range(5)],
                 nbr, [rbk[5 + j % 2]])
            o = ot[oi % 2]; ro = rot[oi % 2]
            oi += 1
            P.op("act", ACTF(o[:], bc, AF.Silu, bias=C.spt[:, BCB + fc:BCB + fc + 1]), [rbk[5 + j % 2], C.rsp], [ro])
            P.dma("act", C.xbc_d[2 * j:2 * j + 2, :, fc, :].rearrange("s p t -> p s t"),
                  o[:].rearrange("p (s t) -> p s t", s=2), [ro], [C.rxbc[2 * j], C.rxbc[2 * j + 1]])

    for lt in range(32):
        b = 0
        ls = slice(lt * 128, (lt + 1) * 128)
        for nb in range(4):
            bi = nb % 2
            P.pe([MM(P.bank(bi), uT[:, kc, ls], Wz[:, kc, nb * 512:(nb + 1) * 512], kc == 0, kc == 7) for kc in range(8)],
                 [ruT[lt // 4], rWz], [rbk[bi]])
            P.op("act", ACTF(zt[b][:, nb * 512:(nb + 1) * 512], P.bank(bi), AF.Silu), [rbk[bi]], [rzt[b]])
        P.pe([MM(P.bank(2)[:, 0:64], uT[:, kc, ls], Wdt[:, kc, :], kc == 0, kc == 7) for kc in range(8)],
             [ruT[lt // 4], rWz], [rbk[2]])
        P.op("dve", TT(dt_all[:, lt, :], P.bank(2)[:, 0:64], rowp[:, DTB:DTB + 64], ALU.add), [rbk[2], rrowp], [rdt])
        P.dma("act", C.zs_d[ls, :], zt[b][:], [rzt[b]], [C.rzs[lt // 4]])
    if stop == "A":
        return
    dtf = dt_all[:].rearrange("p c h -> p (c h)")
    P.op("act", ACTF(dtf, dtf, AF.Exp), [rdt], [rdt])
    P.op("act", ACTF(dtf, dtf, AF.Ln, bias=C.ones_f[:, 0:1]), [rdt, C.rc], [rdt])
    if stop == "B":
        return
    P.new_phase(baseM2)
    F32R = mybir.dt.float32r
    Uf = P.sb("Uf", [128, 128], F32); Tf = P.sb("Tf", [128, 128], F32)
    Ub = P.sb("Ub", [128, 128], F32); Tb = P.sb("Tb", [128, 128], F32)
    Trf = P.sb("Trf", [128, 128], F32); Trb = P.sb("Trb", [128, 128], F32)
    rmk = P.res("mk")
    for (m_, cm, stp, op) in ((Uf, 1, -1, ALU.is_gt), (Tf, -1, 1, ALU.is_ge), (Ub, -1, 1, ALU.is_gt), (Tb, 1, -1, ALU.is_ge)):
        P.op("pool", MS(m_[:], 1.0), [], [rmk])
        P.op("pool", lambda e, m_=m_, cm=cm, stp=stp, op=op: e.affine_select(out=m_[:], in_=m_[:], pattern=[[stp, 128]],
                                                                          compare_op=op, fill=0.0, base=0,
                                                                          channel_multiplier=cm), [rmk], [rmk])
    P.op("dve", CP(Trf[:].bitcast(F32R), Tf[:]), [rmk], [rmk])
    P.op("dve", CP(Trb[:].bitcast(F32R), Tb[:]), [rmk], [rmk])
    xr = [P.sb("xr", [128, 32, 128], BF16) for _ in range(2)]; rxr = [P.res("xr") for _ in range(2)]
    zsb = [P.sb("zsb", [128, 2048], BF16) for _ in range(2)]; rzsb = [P.res("zsb") for _ in range(2)]
    ybl = [P.sb("ybl", [128, 2048], F32) for _ in range(2)]; rybl = [P.res("ybl") for _ in range(2)]
    ych = [P.sb("ych", [128, 8, 256], F32) for _ in range(2)]; rych = [P.res("ych") for _ in range(2)]
    xbtm = [P.sb("xbtm", [128, 8, 384], BF16) for _ in range(2)]; rxbtm = [P.res("xbtm") for _ in range(2)]
    H32 = P.sb("H32", [128, 8, 256], F32); rH32 = P.res("H32")
    Hbf = P.sb("Hbf", [128, 8, 256], BF16); rHbf = P.res("Hbf")
    Htmp = P.sb("Htmp", [128, 256], F32); rHtmp = P.res("Htmp")
    lat = [P.sb("lat", [128, 32], F32) for _ in range(2)]; rlat = [P.res("lat") for _ in range(2)]
    dsc = [P.sb("dsc", [128, 3, 32], F32) for _ in range(2)]; rdsc = [P.res("dsc") for _ in range(2)]
    At = [P.sb("At", [128, 4, 128], F32) for _ in range(3)]; rAt = [P.res("At") for _ in range(3)]
    Et = [P.sb("Et", [128, 4, 128], BF16) for _ in range(2)]; rEt = [P.res("Et") for _ in range(2)]
    Mh = [P.sb("Mh", [128, 4, 128], BF16) for _ in range(2)]; rMh = [P.res("Mh") for _ in range(2)]
    CBm = [P.sb("CBm", [128, 128], BF16) for _ in range(2)]; rCBm = [P.res("CBm") for _ in range(2)]
    xdt = [P.sb("xdt", [128, 4, 64], BF16) for _ in range(3)]; rxdt = [P.res("xdt") for _ in range(3)]
    xdd = [P.sb("xdd", [128, 4, 64], BF16) for _ in range(3)]; rxdd = [P.res("xdd") for _ in range(3)]
    ytmp = [P.sb("ytmp", [128, 256], F32) for _ in range(2)]; rytmp = [P.res("ytmp") for _ in range(2)]
    ytm = P.sb("ytm", [128, 2048], BF16); rytm = P.res("ytm")
    gss = P.sb("gss", [128, 8], F32); rgss = P.res("gss")
    yT = [P.sb("yTm", [128, 16, 128], BF16) for _ in range(2)]; ryT = [P.res("yTm") for _ in range(2)]
    TB = alloc_tail(P, 128, False, banks=(6,))
    rTX, rCB, rREL, rY, rHU, rDC = [P.res("m2bk%d" % i) for i in range(6)]
    bTX = P.bank(0).bitcast(BF16)
    bCB, bREL, bY, bHU, bDC = P.bank(1), P.bank(2), P.bank(3), P.bank(4), P.bank(5)
    bTR = P.bank(7).bitcast(BF16)
    rTR = TB["T"]["rnbank"]

    visits = []
    for ps in range(2):
        order = list(range(32)) if ps == 1 else list(range(31, -1, -1))
        for vi, c in enumerate(order):
            visits.append((ps, vi, c))
    steps = [(gv, g) for gv in range(len(visits)) for g in range(8)]

    def vinfo(gv):
        ps, vi, c = visits[gv]
        fwd = ps == 1
        return ps, vi, c, fwd, (0 if fwd else 1), gv % 2, gv % 2

    def v_pre(gv):
        ps, vi, c, fwd, d, vp, sb_ = vinfo(gv)
        st = c // 2
        P.dma_multi("sp", [(xr[sb_][:, q4 * 8:(q4 + 1) * 8, :],
                            C.xbc_d[st][:, q4 * 8:(q4 + 1) * 8, (c % 2) * 128:(c % 2) * 128 + 128]) for q4 in range(4)],
                    [C.rxbc[st]], [rxr[sb_]])
        Ud, Td = (Uf, Tf) if fwd else (Ub, Tb)
        hs0 = d * 32
        P.op("dve", TT(lat[vp][:], dt_all[:, c, hs0:hs0 + 32], aneg[:, hs0:hs0 + 32], ALU.mult), [rdt, rrowp], [rlat[vp]])
        P.pe([MM(bDC[:, 0:32], Td[:], lat[vp][:]), MM(bDC[:, 32:64], C.ones_f[:], lat[vp][:]),
              MM(bDC[:, 64:96], Ud[:], lat[vp][:])], [rlat[vp], rmk, C.rc], [rDC])
        P.op("act", ACTF(dsc[vp][:].rearrange("p a h -> p (a h)"), bDC[:, 0:96], AF.Exp), [rDC], [rdsc[vp]])

    def stepA0(si):
        gv, g = steps[si]
        ps, vi, c, fwd, d, vp, sb_ = vinfo(gv)
        a3 = si % 3
        Ud = Uf if fwd else Ub
        P.op("pool", TT(At[a3][:].bitcast(F32R), Ud[:].unsqueeze(1).to_broadcast([128, 4, 128]),
                        lat[vp][:, 4 * g:4 * g + 4].unsqueeze(2).to_broadcast([128, 4, 128]), ALU.mult),
             [rmk, rlat[vp]], [rAt[a3]])

    def stepA1(si):
        gv, g = steps[si]
        ps, vi, c, fwd, d, vp, sb_ = vinfo(gv)
        gb = si % 2
        g3 = si % 3
        tc = slice(0, 128)
        X, rX = xr[sb_], rxr[sb_]
        xb, rxb = xbtm[vp], rxbtm[vp]
        Ud, Td, Tr = (Uf, Tf, Trf) if fwd else (Ub, Tb, Trb)
        hs0 = d * 32
        P.pe([TR(bTX[:, 0:128], X[:, 2 * g, tc], C.ident_b[:]), TR(bTX[:, 128:256], X[:, 2 * g + 1, tc], C.ident_b[:]),
              TR(bTX[:, 256:384], X[:, 16 + g, tc], C.ident_b[:])], [rX, C.rc], [rTX])
        P.op("act", ACTF(xb[:, g, :], bTX[:, 0:384], AF.Copy), [rTX], [rxb])
        P.pe([MM(bCB[:, 0:128], X[:, 16 + g, tc], X[:, 24 + g, tc])], [rX], [rCB])
        P.op("dve", TT(CBm[gb][:], bCB[:, 0:128], Td[:], ALU.mult), [rCB, rmk], [rCBm[gb]])
        xv = xb[:, g, 0:256].rearrange("p (h q) -> p h q", h=4)
        P.op("pool", TT(xdt[g3][:], xv, dt_all[:, c, hs0 + 4 * g:hs0 + 4 * g + 4].unsqueeze(2).to_broadcast([128, 4, 64]),
                        ALU.mult), [rxb, rdt], [rxdt[g3]])
        P.op("pool", TT(xdd[g3][:], xdt[g3][:], dsc[vp][:, 2, 4 * g:4 * g + 4].unsqueeze(2).to_broadcast([128, 4, 64]),
                        ALU.mult), [rxdt[g3], rdsc[vp]], [rxdd[g3]])

    def stepA2(si):
        gv, g = steps[si]
        ps, vi, c, fwd, d, vp, sb_ = vinfo(gv)
        gb = si % 2
        Ud, Td, Tr = (Uf, Tf, Trf) if fwd else (Ub, Tb, Trb)
        a3 = si % 3
        P.pe([MM(bREL[:, h * 128:(h + 1) * 128], At[a3][:, h, :].bitcast(F32R), Tr[:].bitcast(F32R)) for h in range(4)],
             [rAt[a3], rmk], [rREL])
        P.op("act", ACTF(Et[gb][:].rearrange("p h l -> p (h l)"), bREL, AF.Exp), [rREL], [rEt[gb]])
        P.op("dve", TT(Mh[gb][:], Et[gb][:], CBm[gb][:].unsqueeze(1).to_broadcast([128, 4, 128]), ALU.mult),
             [rEt[gb], rCBm[gb]], [rMh[gb]])

    def stepB(si):
        gv, g = steps[si]
        ps, vi, c, fwd, d, vp, sb_ = vinfo(gv)
        gb = si % 2
        g3 = si % 3
        tc = slice(0, 128)
        X, rX = xr[sb_], rxr[sb_]
        xb, rxb = xbtm[vp], rxbtm[vp]
        xv = xb[:, g, 0:256].rearrange("p (h q) -> p h q", h=4)
        fns = []
        for h in range(4):
            fns.append(MM(bY[:, h * 64:(h + 1) * 64], Mh[gb][:, h, :], xdt[g3][:, h, :], True, not fwd))
            if fwd:
                fns.append(MM(bY[:, h * 64:(h + 1) * 64], Ddiag[:, 4 * g + h, :], xv[:, h, :], False, True))
        fns.append(MM(bY[:, 256:512], X[:, 24 + g, tc], Hbf[:, g, :]))
        fns.append(MM(bHU[:, 0:256], xb[:, g, 256:384], xdd[g3][:].rearrange("p h q -> p (h q)")))
        P.pe(fns, [rMh[gb], rxdt[g3], rxdd[g3], rxb, rX, rHbf, rDd], [rY, rHU])
        dcyb = dsc[vp][:, 0, 4 * g:4 * g + 4].unsqueeze(2).to_broadcast([128, 4, 64])
        P.op("dve", TT(ytmp[gb][:].rearrange("p (h q) -> p h q", h=4), bY[:, 256:512].rearrange("p (h q) -> p h q", h=4),
                       dcyb, ALU.mult), [rY, rdsc[vp]], [rytmp[gb]])
        P.op("dve", TT(ych[vp][:, g, :], bY[:, 0:256], ytmp[gb][:], ALU.add), [rY, rytmp[gb]], [rych[vp]])
        for h in range(4):
            hq = slice(h * 64, (h + 1) * 64)
            P.op("dve", STT(H32[:, g, hq], H32[:, g, hq], dsc[vp][:, 1, 4 * g + h:4 * g + h + 1], bHU[:, hq], ALU.mult, ALU.add),
                 ([rH32] if h == 0 else []) + [rHU, rdsc[vp]], [rH32])

    def stepB_hbf(si):
        gv, g = steps[si]
        P.op("act", ACTF(Hbf[:, g, :], H32[:, g, :], AF.Copy), [rH32], [rHbf])

    def post_stages(gv):
        ps, vi, c, fwd, d, vp, sb_ = vinfo(gv)
        ychf = ych[vp][:].rearrange("p g q -> p (g q)")
        yb = ybl[vp]

        def e0():
            P.op("pool", TT(ychf, ychf, yb[:], ALU.add), [rych[vp], rybl[vp]], [rych[vp]])

        def e1():
            P.op("dve", TT(yb[:], ychf, zsb[vp][:], ALU.mult), [rych[vp], rzsb[vp]], [rybl[vp]])

        def e2():
            P.op("act", ACTF(ychf, yb[:], AF.Square), [rybl[vp]], [rych[vp]])

        def e3():
            P.op("dve", lambda e: e.tensor_reduce(out=gss[:], in_=ych[vp][:], axis=AX.X, op=ALU.add), [rych[vp]], [rgss])
            P.op("act", ACTF(gss[:], gss[:], AF.Ln, scale=1.0 / 256, bias=C.eps_col[:, 0:1]), [rgss, C.rc], [rgss])
            P.op("act", ACTF(gss[:], gss[:], AF.Exp, scale=-0.5), [rgss], [rgss])

        def e45(g0):
            for g in range(g0, g0 + 4):
                P.op("dve", STT(ytm[:, g * 256:(g + 1) * 256], yb[:, g * 256:(g + 1) * 256], gss[:, g:g + 1],
                                rowp[:, BNW + g * 256:BNW + (g + 1) * 256], ALU.mult, ALU.mult),
                     [rybl[vp], rgss, rrowp], [rytm])

        def e67(q):
            yTc = yT[vp]
            P.pe([TR(bTR[:, k * 128:(k + 1) * 128], ytm[:, (q * 8 + k) * 128:(q * 8 + k + 1) * 128], C.ident_b[:])
                  for k in range(8)], [rytm, C.rc], [rTR])
            P.op("act", ACTF(yTc[:, q * 8:(q + 1) * 8, :].rearrange("p k t -> p (k t)"), bTR[:, 0:1024], AF.Copy),
                 [rTR], [ryT[vp]])

        st = [e0, e1, e2, e3, lambda: e45(0), lambda: e45(4), lambda: e67(0), lambda: e67(1)]
        st += tail_stages(P, C, yT[vp][:], [ryT[vp]], 16, 128, c * 128, TB, vi, nidx_next, False)
        return st

    pending = []

    def tick():
        while pending and pending[0][0] <= 0:
            it = pending.pop(0)
            it[1]()
        for it in pending:
            it[0] -= 1

    NS = len(steps)
    P.op("pool", MS(H32[:], 0.0), [], [rH32])
    P.op("pool", MS(Hbf[:], 0.0), [], [rHbf])
    v_pre(0)
    stepA0(0)
    stepA0(1)
    stepA0(2)
    stepA1(0)
    stepA1(1)
    stepA2(0)
    for si in range(NS):
        gv, g = steps[si]
        if si + 1 < NS:
            stepA2(si + 1)
        stepB(si)
        if si + 3 < NS:
            gv3, g3_ = steps[si + 3]
            if g3_ == 0:
                v_pre(gv3)
            stepA0(si + 3)
        if si + 2 < NS:
            stepA1(si + 2)
        stepB_hbf(si)
        if g == 2 and visits[gv][0] == 1:
            ps, vi, c, fwd, d, vp, sb_ = vinfo(gv)
            P.dma("sp", zsb[vp][:], C.zs_d[c * 128:(c + 1) * 128, :], [C.rzs[c // 4]], [rzsb[vp]])
            P.dma("sp", ybl[vp][:], C.ybwd_d[c * 128:(c + 1) * 128, :], [C.rybwd[c // 4]], [rybl[vp]])
        if g == 7:
            ps, vi, c = visits[gv]
            if ps == 0:
                vp_ = gv % 2
                P.dma("act", C.ybwd_d[c * 128:(c + 1) * 128, :], ych[vp_][:].rearrange("p g q -> p (g q)"), [rych[vp_]],
                      [C.rybwd[c // 4]])
            else:
                for k, fn in enumerate(post_stages(gv)):
                    pending.append([k, fn])
                pending.sort(key=lambda it: it[0])
            if ps == 0 and vi == 31:
                P.op("pool", MS(H32[:], 0.0), [], [rH32])
                P.op("pool", MS(Hbf[:], 0.0), [], [rHbf])
        tick()
    while pending:
        it = pending.pop(0)
        it[1]()

def build_program(stages, debug=False):
    nc = bass.Bass("TRN2", target_bir_lowering=False)
    C = Ctx()

    def din(name, shape):
        return nc.dram_tensor(name, list(shape), F32, kind="ExternalInput").ap()

    C.xT_d = din("xT", [128, 8, L])
    C.sp_d = din("sp", [128, NSP])
    C.rowp_d = din("rowp", [128, NROW])
    C.a_win_d = din("a_win", [40, 128, 8, 128])
    C.a_wout_d = din("a_wout", [1024, 1024])
    C.f_win_d = din("f_win", [2, 2 * NFC, 128, 8, 128])
    C.f_wout_d = din("f_wout", [2, DFF, 1024])
    C.b_wxbc_d = din("b_wxbc", [32, 128, 8, 128])
    C.b_wz_d = din("b_wz", [128, 8, 2048])
    C.b_wdt_d = din("b_wdt", [128, 8, 64])
    C.b_wout_d = din("b_wout", [2048, 1024])
    C.outT_d = nc.dram_tensor("outT", [128, 8, L], F32, kind="ExternalOutput").ap()
    kind = "ExternalOutput" if debug else "Internal"
    C.hT_d = nc.dram_tensor("hT", [128, 8, L], F32, kind=kind).ap()
    C.uT_d = nc.dram_tensor("uT_scr", [128, 8, L], BF16, kind=kind).ap()
    C.y_d = nc.dram_tensor("y_scr", [NFC, 128, L], BF16, kind="Internal").ap()
    C.rows_d = nc.dram_tensor("rows_scr", [HK, 16, 128, 8, 256], BF16, kind="Internal").ap()
    C.obwd_d = nc.dram_tensor("obwd_scr", [16, 128, 8, 256], F32, kind="Internal").ap()
    C.xbc_d = nc.dram_tensor("xbc_scr", [16, 128, 32, 256], BF16, kind="Internal").ap()
    C.zs_d = nc.dram_tensor("zs_scr", [L, 2048], BF16, kind="Internal").ap()
    C.ybwd_d = nc.dram_tensor("ybwd_scr", [L, 2048], F32, kind="Internal").ap()

    P = Prog(nc)
    C.rh = [P.res("h%d" % j, True) for j in range(8)]
    C.ru = [P.res("u%d" % j, True) for j in range(8)]
    C.ry = [P.res("y%d" % c, True) for c in range(NFC)]
    C.rout = [P.res("o%d" % j, True) for j in range(8)]
    C.rrows = [P.res("rows%d" % j, True) for j in range(16)]
    C.robwd = [P.res("obwd%d" % j, True) for j in range(16)]
    C.rxbc = C.rrows
    C.rzs = C.robwd[0:8]
    C.rybwd = C.robwd[8:16]
    setup_consts(P, C)
    C.eps_col = P.sb("epsc", [128, 1], F32)
    P.op("pool", MS(C.eps_col[:], EPS), [], [C.rc])
    C.base0 = P.sb_ptr

    for st in stages:
        if st[0] == "init":
            phase_init_norm(P, C, st[1])
        elif st[0] == "ffn":
            phase_ffn(P, C, st[1], st[2], st[3])
        elif st[0] == "hgrn":
            phase_hgrn(P, C, st[1])
        elif st[0] == "mamba":
            phase_mamba(P, C, st[1], st[2] if len(st) > 2 else None)
    P.barrier()
    P.emit()
    return nc


def host_pack(inputs):
    f = lambda a: np.ascontiguousarray(a, dtype=np.float32)
    sp = np.zeros((128, NSP), np.float32)
    nws = [inputs["norm1_w"][0], inputs["norm2_w"][0], inputs["norm1_w"][1], inputs["norm2_w"][1], inputs["final_norm_w"]]
    for n, w in enumerate(nws):
        sp[:, NW0 + n * 8:NW0 + (n + 1) * 8] = np.asarray(w).reshape(8, 128).T
    sp[:, LBL:LBL + 24] = np.asarray(inputs["a_lb_logits"]).reshape(3, 8, 128).transpose(2, 0, 1).reshape(128, 24)
    sp[:, ANW] = np.asarray(inputs["a_norm_w"])[0]
    sp[:, FCW:FCW + 132] = np.asarray(inputs["ffn_conv_w"]).reshape(2, 3, NFC, 128).transpose(3, 0, 1, 2).reshape(128, 132)
    sp[:, FCB:FCB + 44] = np.asarray(inputs["ffn_conv_b"]).reshape(2, NFC, 128).transpose(2, 0, 1).reshape(128, 44)
    sp[:, BCW:BCW + 160] = np.asarray(inputs["b_conv_w"])[0].reshape(5, 32, 128).transpose(2, 0, 1).reshape(128, 160)
    sp[:, BCB:BCB + 32] = np.asarray(inputs["b_conv_b"])[0].reshape(32, 128).T
    row = np.zeros((NROW,), np.float32)
    row[DTB:DTB + 64] = np.asarray(inputs["b_dt_bias"])[0].reshape(64)
    row[ALOG:ALOG + 64] = np.asarray(inputs["b_a_log"])[0].reshape(64)
    row[DSK:DSK + 32] = np.asarray(inputs["b_d_skip"])[0]
    row[BNW:BNW + 2048] = np.asarray(inputs["b_norm_w"])[0]
    rowp = np.ascontiguousarray(np.broadcast_to(row[None, :], (128, NROW)))

    def chunked(w, ncols):
        return f(np.asarray(w).reshape(8, 128, ncols // 128, 128).transpose(2, 1, 0, 3))

    bw = np.asarray(inputs["b_w_in"])[0]
    shared = {
        "sp": sp, "rowp": rowp,
        "a_win": chunked(inputs["a_w_in"][0], 5120),
        "a_wout": f(inputs["a_w_out"][0]),
        "f_win": np.stack([chunked(inputs["ffn_w_in"][i], 2 * DFF) for i in range(2)]),
        "f_wout": f(inputs["ffn_w_out"]),
        "b_wxbc": chunked(bw[:, 2048:6144], 4096),
        "b_wz": f(bw[:, 0:2048].reshape(8, 128, 2048).transpose(1, 0, 2)),
        "b_wdt": f(bw[:, 6144:6208].reshape(8, 128, 64).transpose(1, 0, 2)),
        "b_wout": f(inputs["b_w_out"][0]),
    }
    return shared


def to_fm(x2d):
    return np.ascontiguousarray(np.asarray(x2d, dtype=np.float32).T.reshape(8, 128, -1).transpose(1, 0, 2))


def from_fm(a):
    return np.ascontiguousarray(a.transpose(1, 0, 2).reshape(D, -1).T)


FULL_STAGES = [("init", 0), ("hgrn", 1), ("ffn", 0, 2, False), ("mamba", 3), ("ffn", 1, 4, True)]


def kernel(**inputs):
    x = np.asarray(inputs["x"], dtype=np.float32)
    nb = x.shape[0]
    shared = host_pack(inputs)
    nc = build_program(FULL_STAGES)
    in_maps = []
    for b in range(nb):
        m = dict(shared)
        m["xT"] = to_fm(x[b])
        in_maps.append(m)
    res = run_bass_kernel_spmd(nc, in_maps, core_ids=list(range(nb)))
    out = np.stack([from_fm(np.asarray(r["outT"])) for r in res.results], axis=0)
    return out.astype(np.float32)
```

```python
import numpy as np
import concourse.bass as bass
import concourse.mybir as mybir
from concourse.bass_utils import run_bass_kernel_spmd
from contextlib import ExitStack

F32 = mybir.dt.float32
BF16 = mybir.dt.bfloat16
ALU = mybir.AluOpType
AF = mybir.ActivationFunctionType
AX = mybir.AxisListType

L = 4096
D = 1024
DFF = 2816
NFC = 22
EPS = 1e-6
SB_LO = 16512
SB_HI = 229344

NW0, LBL, ANW, FCW, FCB, BCW, BCB, NSP = 0, 40, 64, 65, 197, 241, 401, 433
DTB, ALOG, DSK, BNW, NROW = 0, 64, 128, 160, 2208


class Slot:
    __slots__ = ("sem", "val", "busy")

    def __init__(self, sem):
        self.sem = sem
        self.val = 0
        self.busy = False


class Res:
    __slots__ = ("name", "w", "r", "slot")

    def __init__(self, name):
        self.name = name
        self.w = None
        self.r = {}
        self.slot = None


class Prog:
    ENG = ("pe", "act", "dve", "pool", "sp")

    def __init__(self, nc, nslots=88):
        self.nc = nc
        self.st = ExitStack()
        self.q = {e: [] for e in self.ENG}
        self.cnt = {e: 0 for e in self.ENG}
        self.waited = {e: {} for e in self.ENG}
        self.esem = {e: self.st.enter_context(nc.semaphore("s_" + e)) for e in self.ENG}
        self.slots = [Slot(self.st.enter_context(nc.semaphore("d%d" % i))) for i in range(nslots)]
        self.phase_res = []
        self.nname = 0
        self.sb_ptr = SB_LO
        self.psum = nc.alloc_psum_tensor("psum_all", [128, 8, 512], F32)

    def sb(self, name, shape, dt=F32):
        nbytes = int(np.prod(shape[1:])) * (4 if dt == F32 else 2)
        off = (self.sb_ptr + 63) // 64 * 64
        assert off + nbytes <= SB_HI, "SBUF overflow %s: %d" % (name, off + nbytes - SB_HI)
        self.sb_ptr = off + nbytes
        self.nname += 1
        return self.nc.alloc_sbuf_tensor_at("%s_%d" % (name, self.nname), list(shape), dt, offset=off)

    def bank(self, i):
        return self.psum[:, i, :]

    def res(self, name="r", persistent=False):
        r = Res(name)
        if not persistent:
            self.phase_res.append(r)
        return r

    def semh(self, key):
        return self.esem[key] if isinstance(key, str) else key.sem

    def _slot(self, r):
        if r.slot is None:
            for s in self.slots:
                if not s.busy:
                    s.busy = True
                    r.slot = s
                    break
            else:
                raise RuntimeError("out of DMA semaphore slots")
        return r.slot

    def _deps(self, eng, reads, writes):
        deps = {}
        for r in reads:
            if r.w is not None:
                k, v = r.w
                if deps.get(k, 0) < v:
                    deps[k] = v
        for w in writes:
            if w.w is not None:
                k, v = w.w
                if k != eng and deps.get(k, 0) < v:
                    deps[k] = v
            for k, v in w.r.items():
                if k != eng and deps.get(k, 0) < v:
                    deps[k] = v
        return deps

    def _waits(self, eng, deps, skip_self=False):
        wt = self.waited[eng]
        for k, v in deps.items():
            if skip_self and k == eng:
                continue
            if wt.get(k, 0) >= v:
                continue
            wt[k] = v
            self.q[eng].append(("w", k, v))

    def _mark(self, key, t, reads, writes):
        for r in reads:
            if r.r.get(key, 0) < t:
                r.r[key] = t
        for w in writes:
            w.w = (key, t)
            w.r = {}

    def op(self, eng, fn, reads=(), writes=()):
        self._waits(eng, self._deps(eng, reads, writes), skip_self=(eng == "pe"))
        self.cnt[eng] += 1
        t = self.cnt[eng]
        self.q[eng].append(("i", fn, eng))
        self._mark(eng, t, reads, writes)

    def pe(self, fns, reads=(), writes=()):
        self._waits("pe", self._deps("pe", reads, writes), skip_self=True)
        for fn in fns[:-1]:
            self.q["pe"].append(("i", fn, None))
        self.cnt["pe"] += 1
        t = self.cnt["pe"]
        self.q["pe"].append(("i", fns[-1], "pe"))
        self._mark("pe", t, reads, writes)

    def dma(self, queue, out_ap, in_ap, reads, writes, sem_res=None):
        sr = sem_res or writes[0]
        slot = self._slot(sr)
        self._waits(queue, self._deps(queue, reads, writes))
        slot.val += 16
        self.q[queue].append(("d", out_ap, in_ap, slot))
        self._mark(slot, slot.val, reads, writes)

    def dma_multi(self, queue, pairs, reads, writes):
        slot = self._slot(writes[0])
        self._waits(queue, self._deps(queue, reads, writes))
        for (o, i) in pairs:
            slot.val += 16
            self.q[queue].append(("d", o, i, slot))
        self._mark(slot, slot.val, reads, writes)

    def wait_res(self, eng, ress):
        deps = {}
        for r in ress:
            if r.w is not None:
                k, v = r.w
                if deps.get(k, 0) < v:
                    deps[k] = v
        self._waits(eng, deps)

    def barrier(self):
        deps = {e: self.cnt[e] for e in self.ENG if self.cnt[e] > 0}
        for s in self.slots:
            if s.val > 0:
                deps[s] = s.val
        for e in self.ENG:
            self._waits(e, dict(deps))

    def new_phase(self, base):
        self.barrier()
        for r in self.phase_res:
            if r.slot is not None:
                r.slot.busy = False
                r.slot = None
        self.phase_res = []
        self.sb_ptr = base

    def emit(self):
        nc = self.nc
        q = self.q
        semh = self.semh

        def run(eh, items):
            for it in items:
                if it[0] == "w":
                    eh.wait_ge(semh(it[1]), it[2])
                elif it[0] == "i":
                    ins = it[1](eh)
                    if it[2] is not None:
                        ins.then_inc(self.esem[it[2]], 1)
                else:
                    eh.dma_start(out=it[1], in_=it[2]).then_inc(it[3].sem, 16)

        with nc.Block() as block:
            @block.sync
            def _(e):
                run(e, q["sp"])

            @block.scalar
            def _(e):
                run(e, q["act"])

            @block.vector
            def _(e):
                run(e, q["dve"])

            @block.gpsimd
            def _(e):
                run(e, q["pool"])

            @block.tensor
            def _(e):
                run(e, q["pe"])
        self.st.close()


def MM(out, lhsT, rhs, start=True, stop=True):
    return lambda e: e.matmul(out, lhsT=lhsT, rhs=rhs, start=start, stop=stop)


def TR(out, in_, ident):
    return lambda e: e.transpose(out=out, in_=in_, identity=ident)


def ACTF(out, in_, func, **kw):
    return lambda e: e.activation(out=out, in_=in_, func=func, **kw)


def TT(out, in0, in1, op):
    return lambda e: e.tensor_tensor(out=out, in0=in0, in1=in1, op=op)


def TS(out, in0, s1, s2, op0, op1=None):
    if op1 is None:
        return lambda e: e.tensor_scalar(out=out, in0=in0, scalar1=s1, scalar2=None, op0=op0)
    return lambda e: e.tensor_scalar(out=out, in0=in0, scalar1=s1, scalar2=s2, op0=op0, op1=op1)


def STT(out, in0, scalar, in1, op0, op1):
    return lambda e: e.scalar_tensor_tensor(out=out, in0=in0, scalar=scalar, in1=in1, op0=op0, op1=op1)


def CP(out, in_):
    return lambda e: e.tensor_copy(out=out, in_=in_)


def MS(ap, v):
    return lambda e: e.memset(ap, v)


class Ctx:
    pass


def setup_consts(P, C):
    nc = P.nc
    C.ident_f = P.sb("identf", [128, 128], F32)
    C.ident_b = P.sb("identb", [128, 128], BF16)
    C.ones_f = P.sb("onesf", [128, 128], F32)
    C.spt = P.sb("spt", [128, NSP], F32)
    C.rc = P.res("consts", True)
    C.rsp = P.res("sp", True)
    P.dma("sp", C.spt[:], C.sp_d, [], [C.rsp])
    P.op("pool", MS(C.ones_f[:], 1.0), [], [C.rc])
    P.op("pool", MS(C.ident_f[:], 0.0), [], [C.rc])
    P.op("pool", lambda e: e.affine_select(out=C.ident_f[:], in_=C.ident_f[:], pattern=[[-1, 128]],
                                           compare_op=ALU.not_equal, fill=1.0, base=0, channel_multiplier=1),
         [C.rc], [C.rc])
    P.op("pool", CP(C.ident_b[:], C.ident_f[:]), [C.rc], [C.rc])
    C.wout = P.sb("wout", [128, NFC, 1024], BF16)
    C.rwout = P.res("wout", True)
    C.base0 = P.sb_ptr


def load_wout_chunk(P, C, w_d, c, wst, rwst):
    P.dma("sp", wst[:], w_d[c * 128:(c + 1) * 128, :], [], [rwst])
    P.op("pool", CP(C.wout[:, c, :], wst[:]), [rwst], [C.rwout])


def norm_from_tile(P, C, src, rsrc, W, nwcol, T, emit_out):
    sq, rsq = T["sq"], T["rsq"]
    ss, rss = T["ss"], T["rss"]
    rstd, rrstd = T["rstd"], T["rrstd"]
    nb, rnb = T["nbank"], T["rnbank"]
    P.op("act", ACTF(sq[:, :, 0:W], src, AF.Square), [rsrc], [rsq])
    P.op("dve", lambda e: e.tensor_reduce(out=ss[:, 0:W], in_=sq[:, :, 0:W].rearrange("p k w -> p w k"), axis=AX.X,
                                          op=ALU.add), [rsq], [rss])
    P.pe([MM(nb[:, 0:W], C.ones_f[:], ss[:, 0:W])], [rss, C.rc], [rnb])
    P.op("act", ACTF(rstd[:, 0:W], nb[:, 0:W], AF.Ln, scale=1.0 / D, bias=C.eps_col[:, 0:1]), [rnb, C.rc], [rrstd])
    P.op("act", ACTF(rstd[:, 0:W], rstd[:, 0:W], AF.Exp, scale=-0.5), [rrstd], [rrstd])
    for kc in range(8):
        emit_out(kc, rstd[:, 0:W], [rrstd])


def alloc_norm_tmps(P, W):
    T = {}
    T["sq"] = P.sb("sq", [128, 8, W], F32); T["rsq"] = P.res("sq")
    T["ss"] = P.sb("ss", [128, W], F32); T["rss"] = P.res("ss")
    T["rstd"] = P.sb("rstd", [128, W], F32); T["rrstd"] = P.res("rstd")
    T["nbank"] = P.bank(7); T["rnbank"] = P.res("nbank")
    return T


def phase_init_norm(P, C, nidx):
    P.new_phase(C.base0)
    W = 512
    xt = [P.sb("xt", [128, 8, W], F32) for _ in range(2)]
    rxt = [P.res("xt") for _ in range(2)]
    ut = [P.sb("ut", [128, 8, W], BF16) for _ in range(2)]
    rut = [P.res("ut") for _ in range(2)]
    T = alloc_norm_tmps(P, W)
    for j in range(L // W):
        b = j % 2
        sl = slice(j * W, (j + 1) * W)
        P.dma("sp", xt[b][:], C.xT_d[:, :, sl], [], [rxt[b]])
        P.dma("act", C.hT_d[:, :, sl], xt[b][:], [rxt[b]], [C.rh[j]])

        def out(kc, rstd, rd, b=b):
            P.op("dve", STT(ut[b][:, kc, :], xt[b][:, kc, :], C.spt[:, NW0 + nidx * 8 + kc:NW0 + nidx * 8 + kc + 1], rstd,
                            ALU.mult, ALU.mult), [rxt[b], C.rsp] + rd, [rut[b]])
        norm_from_tile(P, C, xt[b][:], rxt[b], W, None, T, out)
        P.dma("act", C.uT_d[:, :, sl], ut[b][:], [rut[b]], [C.ru[j]])


def load_uT(P, C):
    uT = P.sb("uT", [128, 8, L], BF16)
    ruT = [P.res("uT%d" % j) for j in range(8)]
    for j in range(8):
        sl = slice(j * 512, (j + 1) * 512)
        P.dma("sp", uT[:, :, sl], C.uT_d[:, :, sl], [C.ru[j]], [ruT[j]])
    return uT, ruT


def tail_stages(P, C, yT, ry, K, W, t0, TB, i, nidx, final):
    b = i % 2
    ht, rht = TB["ht"][b], TB["rht"][b]
    ut, rut = TB["ut"][b], TB["rut"][b]
    T = TB["T"]
    jt = t0 // 512
    nbk = len(TB["banks"])
    sq, rsq, ss, rss, rstd, rrstd, nb, rnb = T["sq"], T["rsq"], T["ss"], T["rss"], T["rstd"], T["rrstd"], T["nbank"], T["rnbank"]

    def s_load():
        P.dma("sp", ht[:, :, 0:W], C.hT_d[:, :, t0:t0 + W], [C.rh[jt]], [rht])

    def s_dc(dc):
        bk = P.bank(TB["banks"][dc % nbk])
        rbk = TB["rbk"][dc % nbk]
        P.pe([MM(bk[:, 0:W], C.wout[:, c, dc * 128:(dc + 1) * 128], yT[:, c, :], c == 0, c == K - 1) for c in range(K)],
             [C.rwout] + ry, [rbk])
        P.op("dve", TT(ht[:, dc, 0:W], bk[:, 0:W], ht[:, dc, 0:W], ALU.add), [rbk, rht], [rht])

    def s_store_sq():
        if not final:
            P.dma("act", C.hT_d[:, :, t0:t0 + W], ht[:, :, 0:W], [rht], [C.rh[jt]])
        P.op("act", ACTF(sq[:, :, 0:W], ht[:, :, 0:W], AF.Square), [rht], [rsq])

    def s_red():
        if TB.get("red_pool"):
            P.op("pool", TT(ss[:, 0:W], sq[:, 0, 0:W], sq[:, 1, 0:W], ALU.add), [rsq], [rss])
            for kc in range(2, 8):
                P.op("pool", TT(ss[:, 0:W], ss[:, 0:W], sq[:, kc, 0:W], ALU.add), [rsq, rss], [rss])
        else:
            P.op("dve", lambda e: e.tensor_reduce(out=ss[:, 0:W], in_=sq[:, :, 0:W].rearrange("p k w -> p w k"), axis=AX.X,
                                                  op=ALU.add), [rsq], [rss])

    def s_red_b():
        P.pe([MM(nb[:, 0:W], C.ones_f[:], ss[:, 0:W])], [rss, C.rc], [rnb])

    def s_act():
        P.op("act", ACTF(rstd[:, 0:W], nb[:, 0:W], AF.Ln, scale=1.0 / D, bias=C.eps_col[:, 0:1]), [rnb, C.rc], [rrstd])
        P.op("act", ACTF(rstd[:, 0:W], rstd[:, 0:W], AF.Exp, scale=-0.5), [rrstd], [rrstd])

    def s_out(k0, k1):
        for kc in range(k0, k1):
            col = NW0 + nidx * 8 + kc
            P.op("dve", STT(ut[:, kc, 0:W], ht[:, kc, 0:W], C.spt[:, col:col + 1], rstd[:, 0:W], ALU.mult, ALU.mult),
                 [rht, C.rsp, rrstd], [rut])

    def s_fin():
        if final:
            P.dma("act", C.outT_d[:, :, t0:t0 + W], ut[:, :, 0:W], [rut], [C.rout[jt]])
        else:
            P.dma("act", C.uT_d[:, :, t0:t0 + W], ut[:, :, 0:W], [rut], [C.ru[jt]])

    st = [lambda: (s_load(), s_dc(0))]
    for dc in range(1, 8):
        st.append(lambda dc=dc: s_dc(dc))
    if TB.get("split_norm"):
        st += [s_store_sq, s_red, s_red_b, s_act, lambda: s_out(0, 4), lambda: (s_out(4, 8), s_fin())]
    else:
        st += [s_store_sq, lambda: (s_red(), s_red_b(), s_act()), lambda: s_out(0, 4), lambda: (s_out(4, 8), s_fin())]
    return st


def tail_tile(P, C, yT, ry, K, W, t0, TB, i, nidx, final):
    for f in tail_stages(P, C, yT, ry, K, W, t0, TB, i, nidx, final):
        f()


def alloc_tail(P, W, final, banks=(5, 6)):
    TB = {}
    TB["banks"] = banks
    TB["ht"] = [P.sb("ht", [128, 8, W], F32) for _ in range(2)]
    TB["rht"] = [P.res("ht") for _ in range(2)]
    TB["rbk"] = [P.res("tbk") for _ in range(2)]
    TB["ut"] = [P.sb("utl", [128, 8, W], F32 if final else BF16) for _ in range(2)]
    TB["rut"] = [P.res("utl") for _ in range(2)]
    TB["T"] = alloc_norm_tmps(P, W)
    return TB


def phase_ffn(P, C, layer, nidx_next, final):
    P.new_phase(C.base0)
    uT, ruT = load_uT(P, C)
    wst = [P.sb("wst", [128, 1024], F32) for _ in range(4)]
    rwst = [P.res("wst") for _ in range(4)]
    wbf = [P.sb("wbf", [128, 2, 8, 128], BF16) for _ in range(2)]
    rwbf = [P.res("wbf") for _ in range(2)]
    grow = [P.sb("grow", [128, L + 4], BF16) for _ in range(2)]
    rg = [[P.res("g") for _ in range(8)] for _ in range(2)]
    rgpad = P.res("gpad")
    dg = [P.sb("dg", [128, 3, 128], BF16) for _ in range(2)]
    rdg = [P.res("dg") for _ in range(2)]
    sg = [P.sb("sg", [128, 512], F32) for _ in range(2)]
    rsg = [P.res("sg") for _ in range(2)]
    yrow = [P.sb("yrow", [128, L], BF16) for _ in range(2)]
    ryrow = [P.res("yrow") for _ in range(2)]
    rb = [P.res("fbk%d" % i) for i in range(6)]
    for b in range(2):
        P.op("pool", MS(grow[b][:, 0:2], 0.0), [], [rgpad])
        P.op("pool", MS(grow[b][:, L + 2:L + 4], 0.0), [], [rgpad])
    wi = 0
    for c in range(NFC):
        b = c % 2
        for h2 in range(2):
            s = wi % 4
            wi += 1
            P.dma("sp", wst[s][:], C.f_win_d[layer, h2 * NFC + c].rearrange("p k f -> p (k f)"), [], [rwst[s]])
            P.op("pool", CP(wbf[b][:, h2].rearrange("p k f -> p (k f)"), wst[s][:]), [rwst[s]], [rwbf[b]])
        for k in range(3):
            col = FCW + (layer * 3 + k) * NFC + c
            P.op("dve", TS(dg[b][:, k, :], C.ident_b[:], C.spt[:, col:col + 1], None, ALU.mult), [C.rc, C.rsp], [rdg[b]])
        for j in range(8):
            bk = P.bank(j % 2)
            P.pe([MM(bk, wbf[b][:, 0, kc, :], uT[:, kc, j * 512:(j + 1) * 512], kc == 0, kc == 7) for kc in range(8)],
                 [rwbf[b], ruT[j]], [rb[j % 2]])
            P.op("dve", CP(grow[b][:, 2 + j * 512:2 + (j + 1) * 512], bk), [rb[j % 2]], [rg[b][j]])
        for j in range(8):
            bv = P.bank(2 + j % 2)
            P.pe([MM(bv, wbf[b][:, 1, kc, :], uT[:, kc, j * 512:(j + 1) * 512], kc == 0, kc == 7) for kc in range(8)],
                 [rwbf[b], ruT[j]], [rb[2 + j % 2]])
            bc = P.bank(4 + j % 2)
            nb = [rg[b][jj] for jj in (j - 1, j, j + 1) if 0 <= jj < 8] + [rgpad, rdg[b]]
            P.pe([MM(bc, dg[b][:, k, :], grow[b][:, 1 + k + j * 512:1 + k + (j + 1) * 512], k == 0, k == 2) for k in range(3)],
                 nb, [rb[4 + j % 2]])
            colb = FCB + layer * NFC + c
            P.op("act", ACTF(sg[j % 2][:], bc, AF.Silu, bias=C.spt[:, colb:colb + 1]), [rb[4 + j % 2], C.rsp], [rsg[j % 2]])
            P.op("dve", TT(yrow[b][:, j * 512:(j + 1) * 512], sg[j % 2][:], bv, ALU.mult), [rsg[j % 2], rb[2 + j % 2]],
                 [ryrow[b]])
        P.dma("act", C.y_d[c], yrow[b][:], [ryrow[b]], [C.ry[c]])
        load_wout_chunk(P, C, C.f_wout_d[layer], c, wst[wi % 4], rwst[wi % 4])
        wi += 1
    P.new_phase(C.base0)
    W = 512
    yt = [P.sb("yt", [128, NFC, W], BF16) for _ in range(2)]
    ryt = [P.res("yt") for _ in range(2)]
    TB = alloc_tail(P, W, final)
    TB["red_pool"] = True
    TB["split_norm"] = True
    prev_norm = []
    for i in range(L // W):
        b = i % 2
        t0 = i * W
        P.dma_multi("sp", [(yt[b][:, c0:c1, :], C.y_d[c0:c1, :, t0:t0 + W].rearrange("c p t -> p c t"))
                           for (c0, c1) in ((0, 8), (8, 15), (15, NFC))], list(C.ry), [ryt[b]])
        stg = tail_stages(P, C, yt[b][:], [ryt[b]], NFC, W, t0, TB, i, nidx_next, final)
        dcs, norms = stg[:8], stg[8:]
        slot_of = {0: 0, 1: 1, 3: 2, 4: 3, 5: 4, 6: 5}
        for k in range(8):
            dcs[k]()
            if prev_norm and k in slot_of:
                prev_norm[slot_of[k]]()
        prev_norm = norms
    for f in prev_norm:
        f()


HK = 8


def phase_hgrn(P, C, nidx_next):
    P.new_phase(C.base0)
    eblt = [P.sb("ebl", [128, 8, 64], F32) for _ in range(2)]
    rebl = [P.res("ebl") for _ in range(2)]
    baseA2 = P.sb_ptr
    uT, ruT = load_uT(P, C)
    lbt = P.sb("lbt", [128, 6, 8], F32)
    rlb = P.res("lb")
    lg = C.spt[:, LBL:LBL + 24].rearrange("p (r h) -> p r h", r=3)
    P.op("dve", TT(lbt[:, 3, :], lg[:, 0, :], lg[:, 1, :], ALU.max), [C.rsp], [rlb])
    P.op("dve", TT(lbt[:, 3, :], lbt[:, 3, :], lg[:, 2, :], ALU.max), [C.rsp, rlb], [rlb])
    et = P.sb("et", [128, 3, 8], F32)
    for r in range(3):
        P.op("dve", TT(et[:, r, :], lg[:, r, :], lbt[:, 3, :], ALU.subtract), [C.rsp, rlb], [rlb])
    P.op("act", ACTF(et[:], et[:], AF.Exp), [rlb], [rlb])
    P.op("dve", TT(lbt[:, 4, :], et[:, 0, :], et[:, 1, :], ALU.add), [rlb], [rlb])
    P.op("dve", TT(lbt[:, 4, :], lbt[:, 4, :], et[:, 2, :], ALU.add), [rlb], [rlb])
    P.op("dve", lambda e: e.reciprocal(out=lbt[:, 5, :], in_=lbt[:, 4, :]), [rlb], [rlb])
    P.op("dve", TT(lbt[:, 0, :], et[:, 0, :], lbt[:, 5, :], ALU.mult), [rlb], [rlb])
    P.op("dve", TS(lbt[:, 1, :], lbt[:, 0, :], -1.0, 1.0, ALU.mult, ALU.add), [rlb], [rlb])
    P.op("dve", TS(lbt[:, 2, :], lbt[:, 1, :], -1.0, None, ALU.mult), [rlb], [rlb])

    wst = [P.sb("wst", [128, 1024], F32) for _ in range(2)]
    rwst = [P.res("wst") for _ in range(2)]
    wbf = [P.sb("wbf", [128, 5, 8, 128], BF16) for _ in range(2)]
    rwbf = [P.res("wbf") for _ in range(2)]
    W = 512
    mk01 = [P.sb("mk01", [128, W], F32) for _ in range(2)]; rmk01 = P.res("mk01")
    qs = [P.sb("qs", [128, W], F32) for _ in range(3)]; rqs = [P.res("qs") for _ in range(3)]
    gsg = [P.sb("gsg", [128, W], F32) for _ in range(1)]; rgsg = [P.res("gsg") for _ in range(1)]

    def mk(name):
        return ([[P.sb(name, [128, W], F32) for _ in range(2)] for _ in range(2)],
                [[P.res(name) for _ in range(2)] for _ in range(2)])
    sgt = [[P.sb("sgt", [128, W], F32) for _ in range(3)] for _ in range(2)]
    rsgt = [[P.res("sgt") for _ in range(3)] for _ in range(2)]
    ft, rft = mk("ft")
    bt, rbt = mk("bt")
    ebt, rebt = mk("ebt")
    ob = [P.sb("ob", [128, 6, W], BF16) for _ in range(2)]; rob = [P.res("ob") for _ in range(2)]
    obx = [P.sb("obx", [128, 2, W], BF16) for _ in range(2)]; robx = [P.res("obx") for _ in range(2)]
    rbk = [P.res("abk%d" % i) for i in range(8)]
    for d in range(2):
        P.op("pool", MS(mk01[d][:], 1.0), [], [rmk01])
        pos = 0 if d == 0 else 63
        P.op("pool", MS(mk01[d][:].rearrange("p (c t) -> p c t", t=64)[:, :, pos:pos + 1], 0.0), [rmk01], [rmk01])
    wi = [0]
    tiles = [(hd, j) for hd in range(8) for j in range(8)]

    def load_w(hd):
        hb = hd % 2
        for i in range(5):
            s = wi[0] % 2
            wi[0] += 1
            P.dma("sp", wst[s][:], C.a_win_d[i * 8 + hd].rearrange("p k f -> p (k f)"), [], [rwst[s]])
            P.op("pool", CP(wbf[hb][:, i].rearrange("p k f -> p (k f)"), wst[s][:]), [rwst[s]], [rwbf[hb]])
        s = wi[0] % 2
        wi[0] += 1
        load_wout_chunk(P, C, C.a_wout_d, hd, wst[s], rwst[s])

    def stageX(ti):
        hd, j = tiles[ti]
        hb = hd % 2
        tb = ti % 2
        t3 = ti % 3
        if j == 0 and hd == 0:
            load_w(0)
        if j == 1 and hd + 1 < 8:
            load_w(hd + 1)
        bks = []
        for i in range(5):
            bi = i if (i >= 3 or tb == 0) else 5 + i
            bks.append(bi)
            P.pe([MM(P.bank(bi), wbf[hb][:, i, kc, :], uT[:, kc, j * W:(j + 1) * W], kc == 0, kc == 7) for kc in range(8)],
                 [rwbf[hb], ruT[j]], [rbk[bi]])
        o, ro = obx[tb], robx[tb]
        P.op("act", ACTF(qs[t3][:], P.bank(bks[0]), AF.Sigmoid), [rbk[bks[0]]], [rqs[t3]])
        P.op("act", ACTF(gsg[0][:], P.bank(bks[4]), AF.Sigmoid), [rbk[bks[4]]], [rgsg[0]])
        for d in range(2):
            P.op("act", ACTF(sgt[d][t3][:], P.bank(bks[1 + d]), AF.Sigmoid), [rbk[bks[1 + d]]], [rsgt[d][t3]])
        P.op("act", ACTF(o[:, 0, :], P.bank(bks[3]), AF.Copy), [rbk[bks[3]]], [ro])
        P.op("dve", TT(qs[t3][:], P.bank(bks[0]), qs[t3][:], ALU.mult), [rbk[bks[0]], rqs[t3]], [rqs[t3]])
        P.op("dve", TT(o[:, 1, :], P.bank(bks[4]), gsg[0][:], ALU.mult), [rbk[bks[4]], rgsg[0]], [ro])
        for s2 in range(2):
            dst = C.rows_d[6:8, 2 * j + s2, :, hd, :].rearrange("k p t -> p k t")
            P.dma("sp", dst, o[:, :, s2 * 256:(s2 + 1) * 256], [ro], [C.rrows[2 * j + s2]])
        for d in range(2):
            sg = sgt[d][t3]
            P.op("pool", TS(ft[d][tb][:], sg[:], lbt[:, 1, hd:hd + 1], lbt[:, 0, hd:hd + 1], ALU.mult, ALU.add),
                 [rsgt[d][t3], rlb], [rft[d][tb]])
            P.op("pool", TS(sg[:], sg[:], lbt[:, 2, hd:hd + 1], lbt[:, 1, hd:hd + 1], ALU.mult, ALU.add),
                 [rsgt[d][t3], rlb], [rsgt[d][t3]])

    def stageY(ti):
        hd, j = tiles[ti]
        tb = ti % 2
        t3 = ti % 3
        o, ro = ob[tb], rob[tb]
        for d in range(2):
            P.op("act", ACTF(ft[d][tb][:], ft[d][tb][:], AF.Ln), [rft[d][tb]], [rft[d][tb]])
        for d in range(2):
            f_, b_ = ft[d][tb], bt[d][tb]
            if d == 0:
                P.op("dve", lambda e, f_=f_, b_=b_: e.tensor_tensor_scan(out=b_[:], data0=mk01[0][:], data1=f_[:], initial=0.0,
                                                                       op0=ALU.mult, op1=ALU.add),
                     [rft[d][tb], rmk01], [rbt[d][tb]])
            else:
                P.op("dve", lambda e, f_=f_, b_=b_: e.tensor_tensor_scan(out=b_[:, ::-1], data0=mk01[1][:, ::-1],
                                                                       data1=f_[:, ::-1], initial=0.0,
                                                                       op0=ALU.mult, op1=ALU.add),
                     [rft[d][tb], rmk01], [rbt[d][tb]])
        for d in range(2):
            P.op("act", ACTF(ebt[d][tb][:], bt[d][tb][:], AF.Exp), [rbt[d][tb]], [rebt[d][tb]])
            P.op("act", ACTF(bt[d][tb][:], bt[d][tb][:], AF.Exp, scale=-1.0), [rbt[d][tb]], [rbt[d][tb]])
        for d in range(2):
            eb_, reb_ = ebt[d][tb], rebt[d][tb]
            lastpos = 63 if d == 0 else 0
            ebv = eb_[:].rearrange("p (c t) -> p c t", t=64)
            P.op("pool", CP(eblt[d][:, hd, j * 8:(j + 1) * 8].unsqueeze(2), ebv[:, :, lastpos:lastpos + 1]),
                 [reb_], [rebl[d]])
            P.op("pool", TT(o[:, 3 * d + 0, :], qs[t3][:], eb_[:], ALU.mult), [rqs[t3], reb_], [ro])
            P.op("dve", TT(o[:, 3 * d + 1, :], sgt[d][t3][:], bt[d][tb][:], ALU.mult), [rsgt[d][t3], rbt[d][tb]], [ro])
            P.op("dve", TT(o[:, 3 * d + 2, :].rearrange("p (c t) -> p c t", t=64),
                           o[:, 3 * d + 1, :].rearrange("p (c t) -> p c t", t=64),
                           ebv[:, :, lastpos:lastpos + 1].to_broadcast([128, 8, 64]), ALU.mult), [ro, reb_], [ro])
        for s2 in range(2):
            dst = C.rows_d[0:6, 2 * j + s2, :, hd, :].rearrange("k p t -> p k t")
            P.dma("sp", dst, o[:, :, s2 * 256:(s2 + 1) * 256], [ro], [C.robwd[2 * j + s2]])

    stageX(0)
    for ti in range(len(tiles)):
        if ti + 1 < len(tiles):
            stageX(ti + 1)
        stageY(ti)

    P.new_phase(baseA2)
    WS = 256
    maskf = P.sb("maskf", [64, 64], F32)
    maskb = P.sb("maskb", [64, 64], F32)
    rmask = P.res("mask")
    P.op("pool", MS(maskf[:], 1.0), [], [rmask])
    P.op("pool", MS(maskb[:], 1.0), [], [rmask])
    P.op("pool", lambda e: e.affine_select(out=maskf[:], in_=maskf[:], pattern=[[1, 64]], compare_op=ALU.is_ge, fill=0.0,
                                           base=0, channel_multiplier=-1), [rmask], [rmask])
    P.op("pool", lambda e: e.affine_select(out=maskb[:], in_=maskb[:], pattern=[[-1, 64]], compare_op=ALU.is_ge, fill=0.0,
                                           base=0, channel_multiplier=1), [rmask], [rmask])
    S32 = P.sb("S32", [128, 8, 128], F32); rS32 = P.res("S32")
    Sbf = P.sb("Sbf", [128, 8, 128], BF16); rSbf = P.res("Sbf")
    Stmp = P.sb("Stmp", [128, 4, 128], F32); rStmp = P.res("Stmp")
    rt = [P.sb("rt", [128, 4, 8, WS], BF16) for _ in range(2)]
    rrt = [P.res("rt") for _ in range(2)]
    kdtm = [P.sb("kdtm", [64, 8, 128], BF16) for _ in range(2)]; rkdtm = [P.res("kdtm") for _ in range(2)]
    vtm = [P.sb("vtm", [64, 8, 128], BF16) for _ in range(2)]; rvtm = [P.res("vtm") for _ in range(2)]
    attm = [P.sb("attm", [64, 8, 64], BF16) for _ in range(2)]; rattm = [P.res("attm") for _ in range(2)]
    obt = [P.sb("obt", [128, 8, WS], F32) for _ in range(2)]; robt = [P.res("obt") for _ in range(2)]
    obl = [P.sb("obl", [128, 8, WS], F32) for _ in range(2)]; robl = [P.res("obl") for _ in range(2)]
    rst = P.sb("rst", [128, 8, WS], F32); rrst = P.res("rst")
    sqh = P.sb("sqh", [128, 8, WS], F32); rsqh = P.res("sqh")
    yT = [P.sb("yT", [128, 8, WS], BF16) for _ in range(2)]; ryT = [P.res("yT") for _ in range(2)]
    gsb = [P.sb("gsb", [128, 8, WS], BF16) for _ in range(2)]; rgsb = [P.res("gsb") for _ in range(2)]
    TB = alloc_tail(P, WS, False, banks=(6,))
    rTK, rTV, rAT, rO, rUA, rUB = [P.res("a2bk%d" % i) for i in range(6)]
    bTK = P.bank(0).bitcast(BF16)
    bTV = P.bank(1).bitcast(BF16)
    bAT, bO, bU = P.bank(2), P.bank(3), [P.bank(4), P.bank(5)]
    rU = [rUA, rUB]

    for ps in range(2):
        fwd = ps == 1
        d = 0 if fwd else 1
        mask = maskf if fwd else maskb
        P.op("pool", MS(S32[:], 0.0), [], [rS32])
        P.op("pool", MS(Sbf[:], 0.0), [], [rSbf])
        chunks = []
        sts = list(range(16)) if fwd else list(range(15, -1, -1))
        for si, st in enumerate(sts):
            ccs = list(range(4)) if fwd else list(range(3, -1, -1))
            for cc in ccs:
                chunks.append((si, st, cc))
        NCH = len(chunks)

        def load_st(si, st):
            b = si % 2
            if fwd:
                P.dma("sp", rt[b][:, 0:3].rearrange("p k h t -> p k (h t)"),
                      C.rows_d[0:3, st].rearrange("k p h t -> p k (h t)"), [C.rrows[st]], [rrt[b]])
                P.dma("sp", rt[b][:, 3:4].rearrange("p k h t -> p k (h t)"),
                      C.rows_d[6:7, st].rearrange("k p h t -> p k (h t)"), [C.rrows[st]], [rrt[b]])
                P.dma("sp", obl[b][:], C.obwd_d[st], [C.robwd[st]], [robl[b]])
            else:
                P.dma("sp", rt[b][:, 0:4].rearrange("p k h t -> p k (h t)"),
                      C.rows_d[3:7, st].rearrange("k p h t -> p k (h t)"), [C.rrows[st]], [rrt[b]])

        def front(n):
            si, st, cc = chunks[n]
            b = si % 2
            nb = n % 2
            tc = slice(cc * 64, (cc + 1) * 64)
            R = rt[b]
            P.pe([TR(bTK[0:64, hd * 128:(hd + 1) * 128], R[:, 2, hd, tc], C.ident_b[:]) for hd in range(8)],
                 [rrt[b], C.rc], [rTK])
            P.op("act", ACTF(kdtm[nb][:].rearrange("s h k -> s (h k)"), bTK[0:64, :], AF.Copy), [rTK], [rkdtm[nb]])
            P.pe([TR(bTV[0:64, hd * 128:(hd + 1) * 128], R[:, 3, hd, tc], C.ident_b[:]) for hd in range(8)],
                 [rrt[b], C.rc], [rTV])
            P.op("act", ACTF(vtm[nb][:].rearrange("s h k -> s (h k)"), bTV[0:64, :], AF.Copy), [rTV], [rvtm[nb]])
            P.pe([MM(bAT[0:64, hd * 64:(hd + 1) * 64], R[:, 1, hd, tc], R[:, 0, hd, tc]) for hd in range(8)],
                 [rrt[b]], [rAT])
            P.op("dve", TT(attm[nb][:], bAT[0:64, :].rearrange("s (h t) -> s h t", h=8),
                           mask[:].unsqueeze(1).to_broadcast([64, 8, 64]), ALU.mult), [rAT, rmask], [rattm[nb]])

        def back(n):
            si, st, cc = chunks[n]
            b = si % 2
            nb = n % 2
            tc = slice(cc * 64, (cc + 1) * 64)
            R = rt[b]
            for half in range(2):
                P.pe([MM(bU[half][:, q * 128:(q + 1) * 128], kdtm[nb][:, half * 4 + q, :], vtm[nb][:, half * 4 + q, :])
                      for q in range(4)], [rkdtm[nb], rvtm[nb]], [rU[half]])
            fns = []
            for hd in range(8):
                fns.append(MM(bO[:, hd * 64:(hd + 1) * 64], vtm[nb][:, hd, :], attm[nb][:, hd, :], True, False))
                fns.append(MM(bO[:, hd * 64:(hd + 1) * 64], Sbf[:, hd, :], R[:, 0, hd, tc], False, True))
            P.pe(fns, [rvtm[nb], rattm[nb], rSbf, rrt[b]], [rO])
            gch = st * 4 + cc
            for half in range(2):
                hs = slice(half * 4, half * 4 + 4)
                P.op("dve", TT(Stmp[:], S32[:, hs, :], eblt[d][:, hs, gch:gch + 1].to_broadcast([128, 4, 128]), ALU.mult),
                     [rS32, rebl[d]], [rStmp])
                P.op("dve", TT(S32[:, hs, :], bU[half].rearrange("k (h v) -> k h v", h=4), Stmp[:], ALU.add),
                     [rU[half], rStmp], [rS32])
            P.op("act", ACTF(Sbf[:], S32[:], AF.Copy), [rS32], [rSbf])
            oview = bO.rearrange("v (h t) -> v h t", h=8)
            if fwd:
                P.op("dve", TT(obt[b][:, :, tc], oview, obl[b][:, :, tc], ALU.add), [rO, robl[b]], [robt[b]])
            else:
                P.op("act", ACTF(obt[b][:, :, tc], oview, AF.Copy), [rO], [robt[b]])

        def finish_stages(si, st):
            b = si % 2
            o = obt[b]
            nbk = TB["T"]["nbank"]
            rnbk = TB["T"]["rnbank"]

            def f0():
                P.op("act", ACTF(sqh[:], o[:], AF.Square), [robt[b]], [rsqh])
                P.dma("sp", gsb[b][:], C.rows_d[7, st], [C.rrows[st]], [rgsb[b]])

            def f1(h2):
                P.pe([MM(nbk[:, q * WS:(q + 1) * WS], C.ones_f[:], sqh[:, h2 * 2 + q, :]) for q in range(2)],
                     [rsqh, C.rc], [rnbk])
                P.op("act", ACTF(rst[:, h2 * 2:h2 * 2 + 2, :].rearrange("p h t -> p (h t)"), nbk, AF.Ln, scale=1.0 / 128,
                                 bias=C.eps_col[:, 0:1]), [rnbk, C.rc], [rrst])

            def f5():
                P.op("act", ACTF(rst[:], rst[:], AF.Exp, scale=-0.5), [rrst], [rrst])
                P.op("pool", TT(rst[:], rst[:], gsb[b][:], ALU.mult), [rrst, rgsb[b]], [rrst])

            def f6():
                P.op("dve", STT(yT[b][:], o[:], C.spt[:, ANW:ANW + 1], rst[:], ALU.mult, ALU.mult),
                     [robt[b], rrst, C.rsp], [ryT[b]])

            stg = [f0] + [lambda h2=h2: f1(h2) for h2 in range(4)] + [f5, f6]
            stg += tail_stages(P, C, yT[b][:], [ryT[b]], 8, WS, st * WS, TB, si, nidx_next, False)
            return stg

        pend = []

        def tick():
            while pend and pend[0][0] <= 0:
                it = pend.pop(0)
                it[1]()
            for it in pend:
                it[0] -= 1

        load_st(0, sts[0])
        front(0)
        for n in range(NCH):
            si, st, cc = chunks[n]
            if n % 4 == 0 and si + 1 < 16:
                load_st(si + 1, sts[si + 1])
            if n + 1 < NCH:
                front(n + 1)
            back(n)
            if n % 4 == 3:
                if not fwd:
                    P.dma("act", C.obwd_d[st], obt[si % 2][:], [robt[si % 2]], [C.robwd[st]])
                else:
                    for k, fn in enumerate(finish_stages(si, st)):
                        pend.append([k // 2, fn])
                    pend.sort(key=lambda it: it[0])
            tick()
        while pend:
            pend.pop(0)[1]()


def phase_mamba(P, C, nidx_next, stop=None):
    P.new_phase(C.base0)
    rowp = P.sb("rowp", [128, NROW], F32); rrowp = P.res("rowp")
    dt_all = P.sb("dt_all", [128, 32, 64], F32); rdt = P.res("dt_all")
    aneg = P.sb("aneg", [128, 64], F32)
    Ddiag = P.sb("Ddiag", [128, 32, 128], BF16); rDd = P.res("Ddiag")
    baseM2 = P.sb_ptr
    P.dma("sp", rowp[:], C.rowp_d, [], [rrowp])
    P.op("act", ACTF(aneg[:], rowp[:, ALOG:ALOG + 64], AF.Exp), [rrowp], [rrowp])
    P.op("dve", TS(aneg[:], aneg[:], -1.0, None, ALU.mult), [rrowp], [rrowp])
    uT, ruT = load_uT(P, C)
    wst = [P.sb("wst", [128, 1024], F32) for _ in range(2)]
    rwst = [P.res("wst") for _ in range(2)]
    Wz = P.sb("Wz", [128, 8, 2048], BF16); rWz = P.res("Wz")
    Wdt = P.sb("Wdt", [128, 8, 64], BF16)
    wi = 0
    for h in range(32):
        P.op("pool", TT(Ddiag[:, h, :], C.ident_f[:], rowp[:, DSK + h:DSK + h + 1].to_broadcast([128, 128]), ALU.mult),
             [C.rc, rrowp], [rDd])
    zt = [P.sb("zt", [128, 2048], BF16) for _ in range(1)]; rzt = [P.res("zt") for _ in range(1)]
    rbk = [P.res("mbk%d" % i) for i in range(8)]
    wbf = [P.sb("wbf", [128, 8, 128], BF16) for _ in range(2)]; rwbf = [P.res("wbf") for _ in range(2)]
    prow = [P.sb("prow", [128, L + 8], BF16) for _ in range(2)]
    rpr = [[P.res("pr") for _ in range(8)] for _ in range(2)]
    rppad = P.res("ppad")
    dg = [P.sb("dg5", [128, 5, 128], BF16) for _ in range(2)]; rdg = [P.res("dg5") for _ in range(2)]
    ot = [P.sb("xot", [128, 512], BF16) for _ in range(2)]; rot = [P.res("xot") for _ in range(2)]
    for b in range(2):
        P.op("pool", MS(prow[b][:, 0:4], 0.0), [], [rppad])
        P.op("pool", MS(prow[b][:, L + 4:L + 8], 0.0), [], [rppad])
    oi = 0
    for fc in range(32):
        b = fc % 2
        s = wi % 2
        wi += 1
        P.dma("sp", wst[s][:], C.b_wxbc_d[fc].rearrange("p k f -> p (k f)"), [], [rwst[s]])
        P.op("pool", CP(wbf[b][:].rearrange("p k f -> p (k f)"), wst[s][:]), [rwst[s]], [rwbf[b]])
        if fc < 16:
            s = wi % 2
            wi += 1
            load_wout_chunk(P, C, C.b_wout_d, fc, wst[s], rwst[s])
            kc_, hf_ = fc // 2, fc % 2
            s = wi % 2
            wi += 1
            P.dma("sp", wst[s][:], C.b_wz_d[:, kc_, hf_ * 1024:(hf_ + 1) * 1024], [], [rwst[s]])
            P.op("pool", CP(Wz[:, kc_, hf_ * 1024:(hf_ + 1) * 1024], wst[s][:]), [rwst[s]], [rWz])
        if fc == 16:
            s = wi % 2
            wi += 1
            P.dma("sp", wst[s][:, 0:512], C.b_wdt_d.rearrange("p k f -> p (k f)"), [], [rwst[s]])
            P.op("pool", CP(Wdt[:].rearrange("p k f -> p (k f)"), wst[s][:, 0:512]), [rwst[s]], [rWz])
        for k in range(5):
            col = BCW + k * 32 + fc
            P.op("dve", TS(dg[b][:, k, :], C.ident_b[:], C.spt[:, col:col + 1], None, ALU.mult), [C.rc, C.rsp], [rdg[b]])
        for j in range(8):
            bk = P.bank(3 + j % 2)
            P.pe([MM(bk, wbf[b][:, kc, :], uT[:, kc, j * 512:(j + 1) * 512], kc == 0, kc == 7) for kc in range(8)],
                 [rwbf[b], ruT[j]], [rbk[3 + j % 2]])
            P.op("dve", CP(prow[b][:, 4 + j * 512:4 + (j + 1) * 512], bk), [rbk[3 + j % 2]], [rpr[b][j]])
        for j in range(8):
            bc = P.bank(5 + j % 2)
            nbr = [rpr[b][jj] for jj in (j - 1, j, j + 1) if 0 <= jj < 8] + [rppad, rdg[b]]
            P.pe([MM(bc, dg[b][:, k, :], prow[b][:, 2 + k + j * 512:2 + k + (j + 1) * 512], k == 0, k == 4) for k in range(5)],
                 nbr, [rbk[5 + j % 2]])
            o = ot[oi % 2]; ro = rot[oi % 2]
            oi += 1
            P.op("act", ACTF(o[:], bc, AF.Silu, bias=C.spt[:, BCB + fc:BCB + fc + 1]), [rbk[5 + j % 2], C.rsp], [ro])
            P.dma("act", C.xbc_d[2 * j:2 * j + 2, :, fc, :].rearrange("s p t -> p s t"),
                  o[:].rearrange("p (s t) -> p s t", s=2), [ro], [C.rxbc[2 * j], C.rxbc[2 * j + 1]])

    for lt in range(32):
        b = 0
        ls = slice(lt * 128, (lt + 1) * 128)
        for nb in range(4):
            bi = nb % 2
            P.pe([MM(P.bank(bi), uT[:, kc, ls], Wz[:, kc, nb * 512:(nb + 1) * 512], kc == 0, kc == 7) for kc in range(8)],
                 [ruT[lt // 4], rWz], [rbk[bi]])
            P.op("act", ACTF(zt[b][:, nb * 512:(nb + 1) * 512], P.bank(bi), AF.Silu), [rbk[bi]], [rzt[b]])
        P.pe([MM(P.bank(2)[:, 0:64], uT[:, kc, ls], Wdt[:, kc, :], kc == 0, kc == 7) for kc in range(8)],
             [ruT[lt // 4], rWz], [rbk[2]])
        P.op("dve", TT(dt_all[:, lt, :], P.bank(2)[:, 0:64], rowp[:, DTB:DTB + 64], ALU.add), [rbk[2], rrowp], [rdt])
        P.dma("act", C.zs_d[ls, :], zt[b][:], [rzt[b]], [C.rzs[lt // 4]])
    if stop == "A":
        return
    dtf = dt_all[:].rearrange("p c h -> p (c h)")
    P.op("act", ACTF(dtf, dtf, AF.Exp), [rdt], [rdt])
    P.op("act", ACTF(dtf, dtf, AF.Ln, bias=C.ones_f[:, 0:1]), [rdt, C.rc], [rdt])
    if stop == "B":
        return
    P.new_phase(baseM2)
    F32R = mybir.dt.float32r
    Uf = P.sb("Uf", [128, 128], F32); Tf = P.sb("Tf", [128, 128], F32)
    Ub = P.sb("Ub", [128, 128], F32); Tb = P.sb("Tb", [128, 128], F32)
    Trf = P.sb("Trf", [128, 128], F32); Trb = P.sb("Trb", [128, 128], F32)
    rmk = P.res("mk")
    for (m_, cm, stp, op) in ((Uf, 1, -1, ALU.is_gt), (Tf, -1, 1, ALU.is_ge), (Ub, -1, 1, ALU.is_gt), (Tb, 1, -1, ALU.is_ge)):
        P.op("pool", MS(m_[:], 1.0), [], [rmk])
        P.op("pool", lambda e, m_=m_, cm=cm, stp=stp, op=op: e.affine_select(out=m_[:], in_=m_[:], pattern=[[stp, 128]],
                                                                          compare_op=op, fill=0.0, base=0,
                                                                          channel_multiplier=cm), [rmk], [rmk])
    P.op("dve", CP(Trf[:].bitcast(F32R), Tf[:]), [rmk], [rmk])
    P.op("dve", CP(Trb[:].bitcast(F32R), Tb[:]), [rmk], [rmk])
    xr = [P.sb("xr", [128, 32, 128], BF16) for _ in range(2)]; rxr = [P.res("xr") for _ in range(2)]
    zsb = [P.sb("zsb", [128, 2048], BF16) for _ in range(2)]; rzsb = [P.res("zsb") for _ in range(2)]
    ybl = [P.sb("ybl", [128, 2048], F32) for _ in range(2)]; rybl = [P.res("ybl") for _ in range(2)]
    ych = [P.sb("ych", [128, 8, 256], F32) for _ in range(2)]; rych = [P.res("ych") for _ in range(2)]
    xbtm = [P.sb("xbtm", [128, 8, 384], BF16) for _ in range(2)]; rxbtm = [P.res("xbtm") for _ in range(2)]
    H32 = P.sb("H32", [128, 8, 256], F32); rH32 = P.res("H32")
    Hbf = P.sb("Hbf", [128, 8, 256], BF16); rHbf = P.res("Hbf")
    Htmp = P.sb("Htmp", [128, 256], F32); rHtmp = P.res("Htmp")
    lat = [P.sb("lat", [128, 32], F32) for _ in range(2)]; rlat = [P.res("lat") for _ in range(2)]
    dsc = [P.sb("dsc", [128, 3, 32], F32) for _ in range(2)]; rdsc = [P.res("dsc") for _ in range(2)]
    At = [P.sb("At", [128, 4, 128], F32) for _ in range(3)]; rAt = [P.res("At") for _ in range(3)]
    Et = [P.sb("Et", [128, 4, 128], BF16) for _ in range(2)]; rEt = [P.res("Et") for _ in range(2)]
    Mh = [P.sb("Mh", [128, 4, 128], BF16) for _ in range(2)]; rMh = [P.res("Mh") for _ in range(2)]
    CBm = [P.sb("CBm", [128, 128], BF16) for _ in range(2)]; rCBm = [P.res("CBm") for _ in range(2)]
    xdt = [P.sb("xdt", [128, 4, 64], BF16) for _ in range(3)]; rxdt = [P.res("xdt") for _ in range(3)]
    xdd = [P.sb("xdd", [128, 4, 64], BF16) for _ in range(3)]; rxdd = [P.res("xdd") for _ in range(3)]
    ytmp = [P.sb("ytmp", [128, 256], F32) for _ in range(2)]; rytmp = [P.res("ytmp") for _ in range(2)]
    ytm = P.sb("ytm", [128, 2048], BF16); rytm = P.res("ytm")
    gss = P.sb("gss", [128, 8], F32); rgss = P.res("gss")
    yT = [P.sb("yTm", [128, 16, 128], BF16) for _ in range(2)]; ryT = [P.res("yTm") for _ in range(2)]
    TB = alloc_tail(P, 128, False, banks=(6,))
    rTX, rCB, rREL, rY, rHU, rDC = [P.res("m2bk%d" % i) for i in range(6)]
    bTX = P.bank(0).bitcast(BF16)
    bCB, bREL, bY, bHU, bDC = P.bank(1), P.bank(2), P.bank(3), P.bank(4), P.bank(5)
    bTR = P.bank(7).bitcast(BF16)
    rTR = TB["T"]["rnbank"]

    visits = []
    for ps in range(2):
        order = list(range(32)) if ps == 1 else list(range(31, -1, -1))
        for vi, c in enumerate(order):
            visits.append((ps, vi, c))
    steps = [(gv, g) for gv in range(len(visits)) for g in range(8)]

    def vinfo(gv):
        ps, vi, c = visits[gv]
        fwd = ps == 1
        return ps, vi, c, fwd, (0 if fwd else 1), gv % 2, gv % 2

    def v_pre(gv):
        ps, vi, c, fwd, d, vp, sb_ = vinfo(gv)
        st = c // 2
        P.dma_multi("sp", [(xr[sb_][:, q4 * 8:(q4 + 1) * 8, :],
                            C.xbc_d[st][:, q4 * 8:(q4 + 1) * 8, (c % 2) * 128:(c % 2) * 128 + 128]) for q4 in range(4)],
                    [C.rxbc[st]], [rxr[sb_]])
        Ud, Td = (Uf, Tf) if fwd else (Ub, Tb)
        hs0 = d * 32
        P.op("dve", TT(lat[vp][:], dt_all[:, c, hs0:hs0 + 32], aneg[:, hs0:hs0 + 32], ALU.mult), [rdt, rrowp], [rlat[vp]])
        P.pe([MM(bDC[:, 0:32], Td[:], lat[vp][:]), MM(bDC[:, 32:64], C.ones_f[:], lat[vp][:]),
              MM(bDC[:, 64:96], Ud[:], lat[vp][:])], [rlat[vp], rmk, C.rc], [rDC])
        P.op("act", ACTF(dsc[vp][:].rearrange("p a h -> p (a h)"), bDC[:, 0:96], AF.Exp), [rDC], [rdsc[vp]])

    def stepA0(si):
        gv, g = steps[si]
        ps, vi, c, fwd, d, vp, sb_ = vinfo(gv)
        a3 = si % 3
        Ud = Uf if fwd else Ub
        P.op("pool", TT(At[a3][:].bitcast(F32R), Ud[:].unsqueeze(1).to_broadcast([128, 4, 128]),
                        lat[vp][:, 4 * g:4 * g + 4].unsqueeze(2).to_broadcast([128, 4, 128]), ALU.mult),
             [rmk, rlat[vp]], [rAt[a3]])

    def stepA1(si):
        gv, g = steps[si]
        ps, vi, c, fwd, d, vp, sb_ = vinfo(gv)
        gb = si % 2
        g3 = si % 3
        tc = slice(0, 128)
        X, rX = xr[sb_], rxr[sb_]
        xb, rxb = xbtm[vp], rxbtm[vp]
        Ud, Td, Tr = (Uf, Tf, Trf) if fwd else (Ub, Tb, Trb)
        hs0 = d * 32
        P.pe([TR(bTX[:, 0:128], X[:, 2 * g, tc], C.ident_b[:]), TR(bTX[:, 128:256], X[:, 2 * g + 1, tc], C.ident_b[:]),
              TR(bTX[:, 256:384], X[:, 16 + g, tc], C.ident_b[:])], [rX, C.rc], [rTX])
        P.op("act", ACTF(xb[:, g, :], bTX[:, 0:384], AF.Copy), [rTX], [rxb])
        P.pe([MM(bCB[:, 0:128], X[:, 16 + g, tc], X[:, 24 + g, tc])], [rX], [rCB])
        P.op("dve", TT(CBm[gb][:], bCB[:, 0:128], Td[:], ALU.mult), [rCB, rmk], [rCBm[gb]])
        xv = xb[:, g, 0:256].rearrange("p (h q) -> p h q", h=4)
        P.op("pool", TT(xdt[g3][:], xv, dt_all[:, c, hs0 + 4 * g:hs0 + 4 * g + 4].unsqueeze(2).to_broadcast([128, 4, 64]),
                        ALU.mult), [rxb, rdt], [rxdt[g3]])
        P.op("pool", TT(xdd[g3][:], xdt[g3][:], dsc[vp][:, 2, 4 * g:4 * g + 4].unsqueeze(2).to_broadcast([128, 4, 64]),
                        ALU.mult), [rxdt[g3], rdsc[vp]], [rxdd[g3]])

    def stepA2(si):
        gv, g = steps[si]
        ps, vi, c, fwd, d, vp, sb_ = vinfo(gv)
        gb = si % 2
        Ud, Td, Tr = (Uf, Tf, Trf) if fwd else (Ub, Tb, Trb)
        a3 = si % 3
        P.pe([MM(bREL[:, h * 128:(h + 1) * 128], At[a3][:, h, :].bitcast(F32R), Tr[:].bitcast(F32R)) for h in range(4)],
             [rAt[a3], rmk], [rREL])
        P.op("act", ACTF(Et[gb][:].rearrange("p h l -> p (h l)"), bREL, AF.Exp), [rREL], [rEt[gb]])
        P.op("dve", TT(Mh[gb][:], Et[gb][:], CBm[gb][:].unsqueeze(1).to_broadcast([128, 4, 128]), ALU.mult),
             [rEt[gb], rCBm[gb]], [rMh[gb]])

    def stepB(si):
        gv, g = steps[si]
        ps, vi, c, fwd, d, vp, sb_ = vinfo(gv)
        gb = si % 2
        g3 = si % 3
        tc = slice(0, 128)
        X, rX = xr[sb_], rxr[sb_]
        xb, rxb = xbtm[vp], rxbtm[vp]
        xv = xb[:, g, 0:256].rearrange("p (h q) -> p h q", h=4)
        fns = []
        for h in range(4):
            fns.append(MM(bY[:, h * 64:(h + 1) * 64], Mh[gb][:, h, :], xdt[g3][:, h, :], True, not fwd))
            if fwd:
                fns.append(MM(bY[:, h * 64:(h + 1) * 64], Ddiag[:, 4 * g + h, :], xv[:, h, :], False, True))
        fns.append(MM(bY[:, 256:512], X[:, 24 + g, tc], Hbf[:, g, :]))
        fns.append(MM(bHU[:, 0:256], xb[:, g, 256:384], xdd[g3][:].rearrange("p h q -> p (h q)")))
        P.pe(fns, [rMh[gb], rxdt[g3], rxdd[g3], rxb, rX, rHbf, rDd], [rY, rHU])
        dcyb = dsc[vp][:, 0, 4 * g:4 * g + 4].unsqueeze(2).to_broadcast([128, 4, 64])
        P.op("dve", TT(ytmp[gb][:].rearrange("p (h q) -> p h q", h=4), bY[:, 256:512].rearrange("p (h q) -> p h q", h=4),
                       dcyb, ALU.mult), [rY, rdsc[vp]], [rytmp[gb]])
        P.op("dve", TT(ych[vp][:, g, :], bY[:, 0:256], ytmp[gb][:], ALU.add), [rY, rytmp[gb]], [rych[vp]])
        for h in range(4):
            hq = slice(h * 64, (h + 1) * 64)
            P.op("dve", STT(H32[:, g, hq], H32[:, g, hq], dsc[vp][:, 1, 4 * g + h:4 * g + h + 1], bHU[:, hq], ALU.mult, ALU.add),
                 ([rH32] if h == 0 else []) + [rHU, rdsc[vp]], [rH32])

    def stepB_hbf(si):
        gv, g = steps[si]
        P.op("act", ACTF(Hbf[:, g, :], H32[:, g, :], AF.Copy), [rH32], [rHbf])

    def post_stages(gv):
        ps, vi, c, fwd, d, vp, sb_ = vinfo(gv)
        ychf = ych[vp][:].rearrange("p g q -> p (g q)")
        yb = ybl[vp]

        def e0():
            P.op("pool", TT(ychf, ychf, yb[:], ALU.add), [rych[vp], rybl[vp]], [rych[vp]])

        def e1():
            P.op("dve", TT(yb[:], ychf, zsb[vp][:], ALU.mult), [rych[vp], rzsb[vp]], [rybl[vp]])

        def e2():
            P.op("act", ACTF(ychf, yb[:], AF.Square), [rybl[vp]], [rych[vp]])

        def e3():
            P.op("dve", lambda e: e.tensor_reduce(out=gss[:], in_=ych[vp][:], axis=AX.X, op=ALU.add), [rych[vp]], [rgss])
            P.op("act", ACTF(gss[:], gss[:], AF.Ln, scale=1.0 / 256, bias=C.eps_col[:, 0:1]), [rgss, C.rc], [rgss])
            P.op("act", ACTF(gss[:], gss[:], AF.Exp, scale=-0.5), [rgss], [rgss])

        def e45(g0):
            for g in range(g0, g0 + 4):
                P.op("dve", STT(ytm[:, g * 256:(g + 1) * 256], yb[:, g * 256:(g + 1) * 256], gss[:, g:g + 1],
                                rowp[:, BNW + g * 256:BNW + (g + 1) * 256], ALU.mult, ALU.mult),
                     [rybl[vp], rgss, rrowp], [rytm])

        def e67(q):
            yTc = yT[vp]
            P.pe([TR(bTR[:, k * 128:(k + 1) * 128], ytm[:, (q * 8 + k) * 128:(q * 8 + k + 1) * 128], C.ident_b[:])
                  for k in range(8)], [rytm, C.rc], [rTR])
            P.op("act", ACTF(yTc[:, q * 8:(q + 1) * 8, :].rearrange("p k t -> p (k t)"), bTR[:, 0:1024], AF.Copy),
                 [rTR], [ryT[vp]])

        st = [e0, e1, e2, e3, lambda: e45(0), lambda: e45(4), lambda: e67(0), lambda: e67(1)]
        st += tail_stages(P, C, yT[vp][:], [ryT[vp]], 16, 128, c * 128, TB, vi, nidx_next, False)
        return st

    pending = []

    def tick():
        while pending and pending[0][0] <= 0:
            it = pending.pop(0)
            it[1]()
        for it in pending:
            it[0] -= 1

    NS = len(steps)
    P.op("pool", MS(H32[:], 0.0), [], [rH32])
    P.op("pool", MS(Hbf[:], 0.0), [], [rHbf])
    v_pre(0)
    stepA0(0)
    stepA0(1)
    stepA0(2)
    stepA1(0)
    stepA1(1)
    stepA2(0)
    for si in range(NS):
        gv, g = steps[si]
        if si + 1 < NS:
            stepA2(si + 1)
        stepB(si)
        if si + 3 < NS:
            gv3, g3_ = steps[si + 3]
            if g3_ == 0:
                v_pre(gv3)
            stepA0(si + 3)
        if si + 2 < NS:
            stepA1(si + 2)
        stepB_hbf(si)
        if g == 2 and visits[gv][0] == 1:
            ps, vi, c, fwd, d, vp, sb_ = vinfo(gv)
            P.dma("sp", zsb[vp][:], C.zs_d[c * 128:(c + 1) * 128, :], [C.rzs[c // 4]], [rzsb[vp]])
            P.dma("sp", ybl[vp][:], C.ybwd_d[c * 128:(c + 1) * 128, :], [C.rybwd[c // 4]], [rybl[vp]])
        if g == 7:
            ps, vi, c = visits[gv]
            if ps == 0:
                vp_ = gv % 2
                P.dma("act", C.ybwd_d[c * 128:(c + 1) * 128, :], ych[vp_][:].rearrange("p g q -> p (g q)"), [rych[vp_]],
                      [C.rybwd[c // 4]])
            else:
                for k, fn in enumerate(post_stages(gv)):
                    pending.append([k, fn])
                pending.sort(key=lambda it: it[0])
            if ps == 0 and vi == 31:
                P.op("pool", MS(H32[:], 0.0), [], [rH32])
                P.op("pool", MS(Hbf[:], 0.0), [], [rHbf])
        tick()
    while pending:
        it = pending.pop(0)
        it[1]()

def build_program(stages, debug=False):
    nc = bass.Bass("TRN2", target_bir_lowering=False)
    C = Ctx()

    def din(name, shape):
        return nc.dram_tensor(name, list(shape), F32, kind="ExternalInput").ap()

    C.xT_d = din("xT", [128, 8, L])
    C.sp_d = din("sp", [128, NSP])
    C.rowp_d = din("rowp", [128, NROW])
    C.a_win_d = din("a_win", [40, 128, 8, 128])
    C.a_wout_d = din("a_wout", [1024, 1024])
    C.f_win_d = din("f_win", [2, 2 * NFC, 128, 8, 128])
    C.f_wout_d = din("f_wout", [2, DFF, 1024])
    C.b_wxbc_d = din("b_wxbc", [32, 128, 8, 128])
    C.b_wz_d = din("b_wz", [128, 8, 2048])
    C.b_wdt_d = din("b_wdt", [128, 8, 64])
    C.b_wout_d = din("b_wout", [2048, 1024])
    C.outT_d = nc.dram_tensor("outT", [128, 8, L], F32, kind="ExternalOutput").ap()
    kind = "ExternalOutput" if debug else "Internal"
    C.hT_d = nc.dram_tensor("hT", [128, 8, L], F32, kind=kind).ap()
    C.uT_d = nc.dram_tensor("uT_scr", [128, 8, L], BF16, kind=kind).ap()
    C.y_d = nc.dram_tensor("y_scr", [NFC, 128, L], BF16, kind="Internal").ap()
    C.rows_d = nc.dram_tensor("rows_scr", [HK, 16, 128, 8, 256], BF16, kind="Internal").ap()
    C.obwd_d = nc.dram_tensor("obwd_scr", [16, 128, 8, 256], F32, kind="Internal").ap()
    C.xbc_d = nc.dram_tensor("xbc_scr", [16, 128, 32, 256], BF16, kind="Internal").ap()
    C.zs_d = nc.dram_tensor("zs_scr", [L, 2048], BF16, kind="Internal").ap()
    C.ybwd_d = nc.dram_tensor("ybwd_scr", [L, 2048], F32, kind="Internal").ap()

    P = Prog(nc)
    C.rh = [P.res("h%d" % j, True) for j in range(8)]
    C.ru = [P.res("u%d" % j, True) for j in range(8)]
    C.ry = [P.res("y%d" % c, True) for c in range(NFC)]
    C.rout = [P.res("o%d" % j, True) for j in range(8)]
    C.rrows = [P.res("rows%d" % j, True) for j in range(16)]
    C.robwd = [P.res("obwd%d" % j, True) for j in range(16)]
    C.rxbc = C.rrows
    C.rzs = C.robwd[0:8]
    C.rybwd = C.robwd[8:16]
    setup_consts(P, C)
    C.eps_col = P.sb("epsc", [128, 1], F32)
    P.op("pool", MS(C.eps_col[:], EPS), [], [C.rc])
    C.base0 = P.sb_ptr

    for st in stages:
        if st[0] == "init":
            phase_init_norm(P, C, st[1])
        elif st[0] == "ffn":
            phase_ffn(P, C, st[1], st[2], st[3])
        elif st[0] == "hgrn":
            phase_hgrn(P, C, st[1])
        elif st[0] == "mamba":
            phase_mamba(P, C, st[1], st[2] if len(st) > 2 else None)
    P.barrier()
    P.emit()
    return nc


def host_pack(inputs):
    f = lambda a: np.ascontiguousarray(a, dtype=np.float32)
    sp = np.zeros((128, NSP), np.float32)
    nws = [inputs["norm1_w"][0], inputs["norm2_w"][0], inputs["norm1_w"][1], inputs["norm2_w"][1], inputs["final_norm_w"]]
    for n, w in enumerate(nws):
        sp[:, NW0 + n * 8:NW0 + (n + 1) * 8] = np.asarray(w).reshape(8, 128).T
    sp[:, LBL:LBL + 24] = np.asarray(inputs["a_lb_logits"]).reshape(3, 8, 128).transpose(2, 0, 1).reshape(128, 24)
    sp[:, ANW] = np.asarray(inputs["a_norm_w"])[0]
    sp[:, FCW:FCW + 132] = np.asarray(inputs["ffn_conv_w"]).reshape(2, 3, NFC, 128).transpose(3, 0, 1, 2).reshape(128, 132)
    sp[:, FCB:FCB + 44] = np.asarray(inputs["ffn_conv_b"]).reshape(2, NFC, 128).transpose(2, 0, 1).reshape(128, 44)
    sp[:, BCW:BCW + 160] = np.asarray(inputs["b_conv_w"])[0].reshape(5, 32, 128).transpose(2, 0, 1).reshape(128, 160)
    sp[:, BCB:BCB + 32] = np.asarray(inputs["b_conv_b"])[0].reshape(32, 128).T
    row = np.zeros((NROW,), np.float32)
    row[DTB:DTB + 64] = np.asarray(inputs["b_dt_bias"])[0].reshape(64)
    row[ALOG:ALOG + 64] = np.asarray(inputs["b_a_log"])[0].reshape(64)
    row[DSK:DSK + 32] = np.asarray(inputs["b_d_skip"])[0]
    row[BNW:BNW + 2048] = np.asarray(inputs["b_norm_w"])[0]
    rowp = np.ascontiguousarray(np.broadcast_to(row[None, :], (128, NROW)))

    def chunked(w, ncols):
        return f(np.asarray(w).reshape(8, 128, ncols // 128, 128).transpose(2, 1, 0, 3))

    bw = np.asarray(inputs["b_w_in"])[0]
    shared = {
        "sp": sp, "rowp": rowp,
        "a_win": chunked(inputs["a_w_in"][0], 5120),
        "a_wout": f(inputs["a_w_out"][0]),
        "f_win": np.stack([chunked(inputs["ffn_w_in"][i], 2 * DFF) for i in range(2)]),
        "f_wout": f(inputs["ffn_w_out"]),
        "b_wxbc": chunked(bw[:, 2048:6144], 4096),
        "b_wz": f(bw[:, 0:2048].reshape(8, 128, 2048).transpose(1, 0, 2)),
        "b_wdt": f(bw[:, 6144:6208].reshape(8, 128, 64).transpose(1, 0, 2)),
        "b_wout": f(inputs["b_w_out"][0]),
    }
    return shared


def to_fm(x2d):
    return np.ascontiguousarray(np.asarray(x2d, dtype=np.float32).T.reshape(8, 128, -1).transpose(1, 0, 2))


def from_fm(a):
    return np.ascontiguousarray(a.transpose(1, 0, 2).reshape(D, -1).T)


FULL_STAGES = [("init", 0), ("hgrn", 1), ("ffn", 0, 2, False), ("mamba", 3), ("ffn", 1, 4, True)]


def kernel(**inputs):
    x = np.asarray(inputs["x"], dtype=np.float32)
    nb = x.shape[0]
    shared = host_pack(inputs)
    nc = build_program(FULL_STAGES)
    in_maps = []
    for b in range(nb):
        m = dict(shared)
        m["xT"] = to_fm(x[b])
        in_maps.append(m)
    res = run_bass_kernel_spmd(nc, in_maps, core_ids=list(range(nb)))
    out = np.stack([from_fm(np.asarray(r["outT"])) for r in res.results], axis=0)
    return out.astype(np.float32)
```
